# Optimizing a Trainium2 kernel written in Bass

```python
import math
import jax, jax.numpy as jnp
from jax import lax
import numpy as np

D_MODEL = 2048
BATCH = 4
SEQ = 4096
DEPTH = 2

GRID_W = 64
CTX_LEN = 256
N_MIXERS = 2
N_DIRS = 2
RWKV_HEAD = 64
RWKV_HEADS = D_MODEL // RWKV_HEAD
DECAY_LORA = max(32, int(round(1.8 * D_MODEL ** 0.5 / 32)) * 32)
AAA_LORA = max(32, int(round(1.8 * D_MODEL ** 0.5 / 32)) * 32)
GATE_LORA = max(32, int(round(0.6 * D_MODEL ** 0.8 / 32)) * 32)
GN_EPS = 64e-5
L2_EPS = 1e-24
DIFF_HEAD = 64
DIFF_HEADS = D_MODEL // (2 * DIFF_HEAD)
ROPE_BASE = 10000.0
Q_BLOCK = 128
QK_EPS = 1e-6
SUBLN_EPS = 1e-5
D_FF = -(-8 * D_MODEL // (3 * 256)) * 256
NORM_EPS = 1e-6

kernel_name = "hybrid_rwkv7_diffattn_prefix_dit"


def rms_norm(x, g, eps=NORM_EPS):
    xf = x.astype(jnp.float32)
    y = xf * lax.rsqrt(jnp.mean(xf * xf, axis=-1, keepdims=True) + eps)
    return (y * g.astype(jnp.float32)).astype(x.dtype)


def adaln(cvec, w_mod, b_mod):
    m = jax.nn.silu(cvec) @ w_mod + b_mod
    return jnp.split(m[..., None, :], 6, axis=-1)


def modulate(h, shift, scale):
    return h * (1.0 + scale) + shift


def swiglu(h, wg, wu, wd):
    return (jax.nn.silu(h @ wg) * (h @ wu)) @ wd


def centred_shift_delta(x):
    xp = jnp.pad(x, ((0, 0), (1, 1), (0, 0)))
    return 0.5 * (xp[:, :-2] + xp[:, 2:]) - x


def rwkv_inputs(h, mu, wr, wk, wv, w0, w1, w2, a0, a1, a2, g1, g2, kk_scale, ka):
    B, T, _ = h.shape
    f32 = jnp.float32
    heads = lambda t: t.astype(f32).reshape(B, T, RWKV_HEADS, RWKV_HEAD)
    xx = centred_shift_delta(h)
    xr, xw, xk, xv, xa, xg = (h + xx * mu[n] for n in range(6))
    r = heads(xr @ wr)
    k = (xk @ wk).astype(f32)
    v = heads(xv @ wv)
    g = jax.nn.sigmoid(xg @ g1) @ g2
    kk = heads(k * kk_scale)
    kk = kk * lax.rsqrt(jnp.maximum(jnp.sum(kk * kk, axis=-1, keepdims=True), L2_EPS))
    dirs = []
    for d in range(N_DIRS):
        w_log = -jax.nn.softplus(-(w0[d] + jnp.tanh(xw @ w1[d]) @ w2[d]).astype(f32)) - 0.5
        a = jax.nn.sigmoid((a0[d] + (xa @ a1[d]) @ a2[d]).astype(f32))
        k_d = k * (1.0 + (a - 1.0) * ka)
        dirs.append((heads(jnp.exp(-jnp.exp(w_log))), heads(k_d), heads(a)))
    return r, v, g, kk, dirs


def wkv7_scan(s0, r, dec, k, v, a, b, reverse):
    xs = tuple(jnp.moveaxis(t, 1, 0) for t in (r, dec, k, v, a, b))

    def step(S, inp):
        r_t, d_t, k_t, v_t, a_t, b_t = inp
        sa = jnp.einsum('bhvk,bhk->bhv', S, a_t)
        S = S * d_t[:, :, None, :] + sa[..., None] * b_t[:, :, None, :] + v_t[..., None] * k_t[:, :, None, :]
        return S, jnp.einsum('bhvk,bhk->bhv', S, r_t)

    s_final, y = lax.scan(step, s0, xs, reverse=reverse)
    return s_final, jnp.moveaxis(y, 0, 1)


def rwkv_readout(y, r, v, g, k_dirs, r_k, ln_w, ln_b, w_o, dtype):
    B, T, H, N = y.shape
    mean = jnp.mean(y, axis=-1, keepdims=True)
    var = jnp.mean(jnp.square(y - mean), axis=-1, keepdims=True)
    yn = ((y - mean) * lax.rsqrt(var + GN_EPS)).reshape(B, T, H * N) * ln_w + ln_b
    bonus = jnp.sum(r * (k_dirs[0] + k_dirs[1]) * r_k.astype(jnp.float32), axis=-1, keepdims=True) * v
    return ((yn + bonus.reshape(B, T, H * N)) * g).astype(dtype) @ w_o


def rwkv_mixer(h_ctx, h_lat, branch_params, readout_params, need_ctx):
    r_c, v_c, g_c, kk_c, dirs_c = rwkv_inputs(h_ctx, *branch_params)
    r_l, v_l, g_l, kk_l, dirs_l = rwkv_inputs(h_lat, *branch_params)
    B = h_lat.shape[0]
    s0 = jnp.zeros((B, RWKV_HEADS, RWKV_HEAD, RWKV_HEAD), jnp.float32)
    ys_c, ys_l = [], []
    for (dec_c, k_c, a_c), (dec_l, k_l, a_l), rev in zip(dirs_c, dirs_l, (False, True)):
        s_ctx, y_c = wkv7_scan(s0, r_c, dec_c, k_c, v_c, -kk_c, kk_c * a_c, rev)
        _, y_l = wkv7_scan(s_ctx, r_l, dec_l, k_l, v_l, -kk_l, kk_l * a_l, rev)
        ys_c.append(y_c)
        ys_l.append(y_l)
    out_l = rwkv_readout(ys_l[0] + ys_l[1], r_l, v_l, g_l, [d[1] for d in dirs_l], *readout_params, dtype=h_lat.dtype)
    out_c = None
    if need_ctx:
        out_c = rwkv_readout(ys_c[0] + ys_c[1], r_c, v_c, g_c, [d[1] for d in dirs_c], *readout_params, dtype=h_ctx.dtype)
    return out_l, out_c


def axial_angles(n_tokens):
    rows = n_tokens // GRID_W
    row, col = jnp.meshgrid(jnp.arange(rows), jnp.arange(GRID_W), indexing='ij')
    n_freq = DIFF_HEAD // 4
    freqs = ROPE_BASE ** (-jnp.arange(n_freq, dtype=jnp.float32) / n_freq)
    ang_row = row.reshape(-1).astype(jnp.float32)[:, None] * freqs
    ang_col = col.reshape(-1).astype(jnp.float32)[:, None] * freqs
    return ang_row, ang_col


def rotate(u, ang):
    cos = jnp.cos(ang)[:, None, None, :]
    sin = jnp.sin(ang)[:, None, None, :]
    u1, u2 = jnp.split(u, 2, axis=-1)
    return jnp.concatenate([u1 * cos - u2 * sin, u1 * sin + u2 * cos], axis=-1)


def axial_rope(t, ang_row, ang_col):
    half = DIFF_HEAD // 2
    tf = t.astype(jnp.float32)
    out = jnp.concatenate([rotate(tf[..., :half], ang_row), rotate(tf[..., half:], ang_col)], axis=-1)
    return out.astype(t.dtype)


def diff_qkv(h, w_qkv, q_gain, k_gain):
    B, T, _ = h.shape
    q, k, v = jnp.split(h @ w_qkv, 3, axis=-1)
    q = rms_norm(q.reshape(B, T, DIFF_HEADS, 2, DIFF_HEAD), q_gain, QK_EPS)
    k = rms_norm(k.reshape(B, T, DIFF_HEADS, 2, DIFF_HEAD), k_gain, QK_EPS)
    v = v.reshape(B, T, DIFF_HEADS, 2 * DIFF_HEAD)
    return q, k, v


def diff_attend(q, k, v, lam):
    s = jnp.einsum('bqhmd,bkhmd->bhmqk', q, k).astype(jnp.float32) * (DIFF_HEAD ** -0.5)
    p = jax.nn.softmax(s, axis=-1)
    attn = p[:, :, 0] - lam * p[:, :, 1]
    return jnp.einsum('bhqk,bkhe->bqhe', attn.astype(v.dtype), v)


def diff_attn_mixer(h_ctx, h_lat, w_qkv, q_gain, k_gain, lq1, lk1, lq2, lk2, sub_gain, w_o,
                    lambda_init, ang_row, ang_col, need_ctx):
    B, T, _ = h_lat.shape
    q_c, k_c, v_c = diff_qkv(h_ctx, w_qkv, q_gain, k_gain)
    q_l, k_l, v_l = diff_qkv(h_lat, w_qkv, q_gain, k_gain)
    q_l = axial_rope(q_l, ang_row, ang_col)
    k_l = axial_rope(k_l, ang_row, ang_col)
    f32 = jnp.float32
    lam = (jnp.exp(jnp.sum(lq1.astype(f32) * lk1.astype(f32)))
           - jnp.exp(jnp.sum(lq2.astype(f32) * lk2.astype(f32))) + lambda_init)
    k_all = jnp.concatenate([k_c, k_l], axis=1)
    v_all = jnp.concatenate([v_c, v_l], axis=1)
    nb = T // Q_BLOCK
    q_blocks = jnp.moveaxis(q_l.reshape(B, nb, Q_BLOCK, DIFF_HEADS, 2, DIFF_HEAD), 1, 0)
    o_l = lax.map(lambda qb: diff_attend(qb, k_all, v_all, lam), q_blocks)
    o_l = jnp.moveaxis(o_l, 0, 1).reshape(B, T, DIFF_HEADS, 2 * DIFF_HEAD)

    def readout(o):
        o = rms_norm(o, sub_gain, SUBLN_EPS) * (1.0 - lambda_init)
        return o.reshape(o.shape[0], o.shape[1], -1) @ w_o

    out_l = readout(o_l)
    out_c = readout(diff_attend(q_c, k_c, v_c, lam)) if need_ctx else None
    return out_l, out_c


def setup_inputs(seed: int = 0) -> dict:
    key = jax.random.key(seed)
    ks = iter(jax.random.split(key, 48))
    f32 = jnp.float32
    D = D_MODEL
    n_r = (DEPTH + 1) // 2
    n_d = DEPTH // 2
    nrm = lambda shape, scale: jax.random.normal(next(ks), shape, f32) * scale
    ramp = jnp.linspace(0.0, 1.0, D, dtype=f32) ** 0.9
    inp = {}
    inp['x'] = nrm((BATCH, SEQ, D), 1.0)
    inp['c'] = nrm((BATCH, D), 1.0)
    inp['ctx'] = nrm((BATCH, CTX_LEN, D), 1.0)
    inp['c_ctx'] = nrm((D,), 1.0)
    inp['mod_w'] = nrm((DEPTH, D, 6 * D), 0.5 * D ** -0.5)
    inp['mod_b'] = nrm((DEPTH, 6 * D), 0.01)
    inp['norm1_g'] = 1.0 + nrm((DEPTH, D), 0.1)
    inp['norm2_g'] = 1.0 + nrm((DEPTH, D), 0.1)
    inp['rwkv_mu'] = jax.random.uniform(next(ks), (n_r, 6, D), f32)
    inp['rwkv_wr'] = nrm((n_r, D, D), D ** -0.5)
    inp['rwkv_wk'] = nrm((n_r, D, D), D ** -0.5)
    inp['rwkv_wv'] = nrm((n_r, D, D), D ** -0.5)
    inp['rwkv_wo'] = nrm((n_r, D, D), D ** -0.5)
    inp['rwkv_w0'] = (-6.0 + 5.0 * ramp) + nrm((n_r, N_DIRS, D), 0.1)
    inp['rwkv_w1'] = nrm((n_r, N_DIRS, D, DECAY_LORA), D ** -0.5)
    inp['rwkv_w2'] = nrm((n_r, N_DIRS, DECAY_LORA, D), 0.1 * DECAY_LORA ** -0.5)
    inp['rwkv_a0'] = nrm((n_r, N_DIRS, D), 0.1)
    inp['rwkv_a1'] = nrm((n_r, N_DIRS, D, AAA_LORA), D ** -0.5)
    inp['rwkv_a2'] = nrm((n_r, N_DIRS, AAA_LORA, D), 0.1 * AAA_LORA ** -0.5)
    inp['rwkv_g1'] = nrm((n_r, D, GATE_LORA), D ** -0.5)
    inp['rwkv_g2'] = nrm((n_r, GATE_LORA, D), GATE_LORA ** -0.5)
    inp['rwkv_kk'] = 0.85 + nrm((n_r, D), 0.05)
    inp['rwkv_ka'] = 1.0 + nrm((n_r, D), 0.05)
    inp['rwkv_rk'] = nrm((n_r, RWKV_HEADS, RWKV_HEAD), 0.05)
    inp['rwkv_lnw'] = 1.0 + nrm((n_r, D), 0.1)
    inp['rwkv_lnb'] = nrm((n_r, D), 0.01)
    inp['diff_wqkv'] = nrm((n_d, D, 3 * D), D ** -0.5)
    inp['diff_qn'] = 1.0 + nrm((n_d, DIFF_HEAD), 0.1)
    inp['diff_kn'] = 1.0 + nrm((n_d, DIFF_HEAD), 0.1)
    inp['diff_lq1'] = nrm((n_d, DIFF_HEAD), 0.1)
    inp['diff_lk1'] = nrm((n_d, DIFF_HEAD), 0.1)
    inp['diff_lq2'] = nrm((n_d, DIFF_HEAD), 0.1)
    inp['diff_lk2'] = nrm((n_d, DIFF_HEAD), 0.1)
    inp['diff_subln'] = 1.0 + nrm((n_d, 2 * DIFF_HEAD), 0.1)
    inp['diff_wo'] = nrm((n_d, D, D), D ** -0.5)
    inp['ffn_wg'] = nrm((DEPTH, D, D_FF), D ** -0.5)
    inp['ffn_wu'] = nrm((DEPTH, D, D_FF), D ** -0.5)
    inp['ffn_wd'] = nrm((DEPTH, D_FF, D), D_FF ** -0.5)
    return inp


def reference(x, c, ctx, c_ctx, mod_w, mod_b, norm1_g, norm2_g,
              rwkv_mu, rwkv_wr, rwkv_wk, rwkv_wv, rwkv_wo, rwkv_w0, rwkv_w1, rwkv_w2,
              rwkv_a0, rwkv_a1, rwkv_a2, rwkv_g1, rwkv_g2, rwkv_kk, rwkv_ka, rwkv_rk, rwkv_lnw, rwkv_lnb,
              diff_wqkv, diff_qn, diff_kn, diff_lq1, diff_lk1, diff_lq2, diff_lk2, diff_subln, diff_wo,
              ffn_wg, ffn_wu, ffn_wd):
    n_lat = x.shape[1]
    ang_row, ang_col = axial_angles(n_lat)
    xl, xc = x, ctx
    for i in range(DEPTH):
        last = i == DEPTH - 1
        sh1, sc1, gt1, sh2, sc2, gt2 = adaln(c, mod_w[i], mod_b[i])
        csh1, csc1, cgt1, csh2, csc2, cgt2 = adaln(c_ctx, mod_w[i], mod_b[i])
        hl = modulate(rms_norm(xl, norm1_g[i]), sh1, sc1)
        hc = modulate(rms_norm(xc, norm1_g[i]), csh1, csc1)
        j = i // N_MIXERS
        if i % N_MIXERS == 0:
            branch = (rwkv_mu[j], rwkv_wr[j], rwkv_wk[j], rwkv_wv[j], rwkv_w0[j], rwkv_w1[j], rwkv_w2[j],
                      rwkv_a0[j], rwkv_a1[j], rwkv_a2[j], rwkv_g1[j], rwkv_g2[j], rwkv_kk[j], rwkv_ka[j])
            readout = (rwkv_rk[j], rwkv_lnw[j], rwkv_lnb[j], rwkv_wo[j])
            ol, oc = rwkv_mixer(hc, hl, branch, readout, not last)
        else:
            lambda_init = 0.8 - 0.6 * math.exp(-0.3 * i)
            ol, oc = diff_attn_mixer(hc, hl, diff_wqkv[j], diff_qn[j], diff_kn[j], diff_lq1[j], diff_lk1[j],
                                     diff_lq2[j], diff_lk2[j], diff_subln[j], diff_wo[j], lambda_init,
                                     ang_row, ang_col, not last)
        xl = xl + gt1 * ol
        hl2 = modulate(rms_norm(xl, norm2_g[i]), sh2, sc2)
        xl = xl + gt2 * swiglu(hl2, ffn_wg[i], ffn_wu[i], ffn_wd[i])
        if not last:
            xc = xc + cgt1 * oc
            hc2 = modulate(rms_norm(xc, norm2_g[i]), csh2, csc2)
            xc = xc + cgt2 * swiglu(hc2, ffn_wg[i], ffn_wu[i], ffn_wd[i])
    return xl
```

```python
import numpy as np
import ml_dtypes
import concourse.bass as bass
import concourse.mybir as mybir
from concourse.bass_utils import run_bass_kernel_spmd

F32 = mybir.dt.float32
BF16 = mybir.dt.bfloat16
AF = mybir.ActivationFunctionType
ALU = mybir.AluOpType
AX = mybir.AxisListType

D = 2048
DC = 16
NCTX = 256
NOWN = 2048
NT = NCTX + NOWN
NTP = NT + 4
DFF = 5632
FC = 44
LW = 96
LG = 256
H = 32
SEM_LIMIT = 30000
NWB = 3
import os as _os
NO_POOL = bool(_os.environ.get('NO_POOL', '1') == '1')


class Res:
    __slots__ = ("w", "r")

    def __init__(self):
        self.w = {}
        self.r = {}


class KB:
    def __init__(self, nc):
        self.nc = nc
        self.eng = {"pe": nc.tensor, "act": nc.scalar, "dve": nc.vector, "pool": nc.gpsimd, "sp": nc.sync}
        self.cur = {}
        self.nsem = 0
        self.waited = {e: {} for e in self.eng}
        self.slots = {"sp": [], "pool": []}
        self.slot_i = {"sp": 0, "pool": 0}
        for q, n in (("sp", 24), ("pool", 12)):
            for i in range(n):
                self.slots[q].append([self._newsem(f"d{q}{i}"), 0])
        for e in ("pe", "act", "dve", "pool"):
            self.cur[e] = [self._newsem(f"e{e}"), 0]
        self.n_ins = 0

    def _newsem(self, name):
        self.nsem += 1
        return self.nc.alloc_semaphore(f"{name}_{self.nsem}")

    def _wait(self, e, toks):
        w = self.waited[e]
        for key, (sem, val, src) in toks.items():
            if src == e and e == 'pe':
                continue
            if w.get(key, 0) < val:
                self.eng[e].wait_ge(sem, val)
                w[key] = val
                self.n_ins += 1

    @staticmethod
    def _merge(dst, src):
        for k, t in src.items():
            o = dst.get(k)
            if o is None or o[1] < t[1]:
                dst[k] = t

    def _deps(self, reads, writes):
        toks = {}
        for r in reads:
            self._merge(toks, r.w)
        for w in writes:
            self._merge(toks, w.w)
            self._merge(toks, w.r)
        return toks

    def _mark(self, tok, key, reads, writes):
        for r in reads:
            o = r.r.get(key)
            if o is None or o[1] < tok[1]:
                r.r[key] = tok
        for w in writes:
            w.w = {key: tok}
            w.r = {}

    def op(self, e, fn, reads=(), writes=()):
        if e == 'pool' and NO_POOL:
            e = 'dve'
        self._wait(e, self._deps(reads, writes))
        ins = fn(self.eng[e])
        c = self.cur[e]
        if c[1] >= SEM_LIMIT:
            c = self.cur[e] = [self._newsem(f"e{e}"), 0]
        c[1] += 1
        ins.then_inc(c[0], 1)
        self.n_ins += 1
        self._mark((c[0], c[1], e), c[0].name, reads, writes)
        return ins

    def dma(self, q, out, in_, reads=(), writes=(), **kw):
        toks = self._deps(reads, writes)
        sl = self.slots[q]
        i = self.slot_i[q]
        self.slot_i[q] = (i + 1) % len(sl)
        s = sl[i]
        if s[1] >= SEM_LIMIT:
            s[0] = self._newsem(f"d{q}x")
            s[1] = 0
        if s[1] > 0:
            toks[s[0].name] = (s[0], s[1], "dma")
        self._wait(q, toks)
        ins = self.eng[q].dma_start(out=out, in_=in_, **kw)
        s[1] += 16
        ins.then_inc(s[0], 16)
        self.n_ins += 1
        self._mark((s[0], s[1], "dma"), s[0].name, reads, writes)
        return ins

    def all_tokens(self):
        toks = {}
        for e, c in self.cur.items():
            if c[1] > 0:
                toks[c[0].name] = (c[0], c[1], "x")
        for q in self.slots:
            for s in self.slots[q]:
                if s[1] > 0:
                    toks[s[0].name] = (s[0], s[1], "dma")
        return toks

    def barrier(self, engines=("pe", "act", "dve", "pool", "sp")):
        toks = self.all_tokens()
        for e in engines:
            self._wait(e, toks)


def dram_fm(t):
    return t.rearrange("(c p) n -> p c n", p=128)


class Builder:
    def __init__(self, debug=None):
        self.debug = debug or []
        nc = self.nc = bass.Bass("TRN2", target_bir_lowering=False)
        self.kb = KB(nc)
        self.inp = {}
        self.out = {}

    def sbt(self, name, shape, dt):
        self._uid = getattr(self, "_uid", 0) + 1
        return self.nc.sbuf_tensor(f"{name}_u{self._uid}", shape, dt)

    def pst(self, name, shape, dt):
        self._uid = getattr(self, "_uid", 0) + 1
        return self.nc.psum_tensor(f"{name}_u{self._uid}", shape, dt)

    def din(self, name, shape, dt=F32):
        t = self.nc.dram_tensor(name, list(shape), dt, kind="ExternalInput").ap()
        self.inp[name] = t
        return t

    def dscr(self, name, shape, dt=F32):
        kind = "ExternalOutput" if name in self.debug else "Internal"
        t = self.nc.dram_tensor(name, list(shape), dt, kind=kind).ap()
        return t

    def load_consts(self, ident_f, ident_b, vec_all, nv):
        nc, kb = self.nc, self.kb
        self.ident_f = nc.alloc_sbuf_tensor("ident_f", [128, 128], F32).ap()
        self.ident_b = nc.alloc_sbuf_tensor("ident_b", [128, 128], BF16).ap()
        self.r_ident = Res()
        kb.dma("sp", self.ident_f, ident_f, writes=[self.r_ident])
        kb.dma("pool", self.ident_b, ident_f, writes=[self.r_ident])
        nrows = nv * 16
        self.vfm = nc.alloc_sbuf_tensor("vfm", [128, nrows], F32).ap()
        self.r_vfm = Res()
        with self.sbt("vrows", [128, 128], F32) as vrows, self.pst("vps", [128, 128], F32) as vps:
            r_rows, r_ps = Res(), Res()
            for g in range((nrows + 127) // 128):
                n = min(128, nrows - g * 128)
                kb.dma("sp", vrows[0:n, :], vec_all[g * 128:g * 128 + n, :], writes=[r_rows])
                kb.op("pe", lambda e: e.matmul(vps[:, 0:n], vrows[0:n, :], self.ident_f[0:n, 0:n], start=True, stop=True),
                      reads=[r_rows, self.r_ident], writes=[r_ps])
                kb.op("dve", lambda e: e.tensor_copy(self.vfm[:, g * 128:g * 128 + n], vps[:, 0:n]),
                      reads=[r_ps], writes=[self.r_vfm])
            kb.barrier()

    def vcol(self, idx, c):
        j = idx * 16 + c
        return self.vfm[:, j:j + 1]


TT = [(1, 256)] + [(259 + 512 * j, 512) for j in range(4)]


def _B(self):
    return self.nc, self.kb


class Builder2(Builder):
    def rows_to_fm(self, srcs, dst):
        nc, kb = _B(self)
        dv = dram_fm(dst)
        with self.sbt("t_x", [128, 2, 2048], F32) as xt, self.sbt("t_o", [128, 2, 16, 128], F32) as ot, \
                self.pst("t_ps", [128, 4, 4, 128], F32) as ps, self.sbt("t_z", [128, 16, 2], F32) as zt:
            rx = [Res(), Res()]
            ro = [Res(), Res()]
            rp = [Res() for _ in range(4)]
            rz = Res()
            kb.op("dve", lambda e: e.memset(zt[:], 0.0), writes=[rz])
            for pc in (0, 257):
                kb.dma("sp", dv[:, :, pc:pc + 1], zt[:, :, 0:1], reads=[rz], allow_slow_non_contiguous=True)
            it = 0
            for (src, n, pc0) in srcs:
                for r0 in range(0, n, 128):
                    m = min(128, n - r0)
                    b = it % 2
                    kb.dma("sp", xt[0:m, b, :], src[r0:r0 + m, :], writes=[rx[b]])
                    for q in range(4):
                        for j in range(4):
                            c = q * 4 + j
                            kb.op("pe", lambda e: e.matmul(ps[:, q, j, 0:m], xt[0:m, b, c * 128:(c + 1) * 128],
                                                           self.ident_f[0:m, 0:m], start=True, stop=True),
                                  reads=[rx[b], self.r_ident], writes=[rp[q]])
                        eng = "dve" if q % 2 == 0 else "act"
                        if eng == "dve":
                            kb.op("dve", lambda e: e.tensor_copy(ot[:, b, q * 4:(q + 1) * 4, 0:m], ps[:, q, :, 0:m]),
                                  reads=[rp[q]], writes=[ro[b]])
                        else:
                            kb.op("act", lambda e: e.copy(ot[:, b, q * 4:(q + 1) * 4, 0:m], ps[:, q, :, 0:m]),
                                  reads=[rp[q]], writes=[ro[b]])
                    kb.dma("sp", dv[:, :, pc0 + r0:pc0 + r0 + m], ot[:, b, :, 0:m], reads=[ro[b]], allow_slow_non_contiguous=(m < 8))
                    it += 1
            kb.barrier()

    def adaln(self, mod_w, vi):
        nc, kb = _B(self)
        self.modfm = nc.alloc_sbuf_tensor("modfm", [128, 2 * 6 * 16 * 2], F32).ap().rearrange(
            "p (l k c j) -> p l k c j", l=2, k=6, c=16)
        self.r_mod = Res()
        with self.sbt("a_sc", [128, 16, 2], BF16) as sc, self.sbt("a_w", [128, 2, 16, 512], BF16) as wt, \
                self.pst("a_ps", [128, 2, 4, 2], F32) as ps:
            r_sc = Res()
            rw = [Res(), Res()]
            rp = [Res(), Res()]
            for j, nm in enumerate(("c", "cctx")):
                i0 = vi[nm] * 16
                kb.op("act", lambda e: e.activation(out=sc[:, :, j], in_=self.vfm[:, i0:i0 + 16], func=AF.Silu),
                      reads=[self.r_vfm], writes=[r_sc])
            it = 0
            for l in range(2):
                wv = mod_w[l].rearrange("(kc p) n -> p kc n", p=128)
                for g in range(24):
                    b = it % 2
                    kb.dma("pool", wt[:, b], wv[:, :, g * 512:(g + 1) * 512], writes=[rw[b]])
                    for j in range(4):
                        for kc in range(16):
                            kb.op("pe", lambda e: e.matmul(ps[:, b, j, :], wt[:, b, kc, j * 128:(j + 1) * 128], sc[:, kc, :],
                                                           start=(kc == 0), stop=(kc == 15)),
                                  reads=[rw[b], r_sc], writes=[rp[b]])
                    for j in range(4):
                        oc = g * 4 + j
                        k, cc = oc // 16, oc % 16
                        kb.op("dve", lambda e: e.tensor_scalar(self.modfm[:, l, k, cc, :], ps[:, b, j, :],
                                                               self.vcol(vi[f"modb{l}_{k}"], cc), None, ALU.add),
                              reads=[rp[b], self.r_vfm], writes=[self.r_mod])
                    it += 1
            for l in range(2):
                for (k, nm) in ((1, f"n1g{l}"), (4, f"n2g{l}")):
                    i0 = vi[nm] * 16
                    for j in range(2):
                        kb.op("dve", lambda e: e.scalar_tensor_tensor(
                            out=self.modfm[:, l, k, :, j], in0=self.modfm[:, l, k, :, j], scalar=1.0,
                            in1=self.vfm[:, i0:i0 + 16], op0=ALU.add, op1=ALU.mult),
                            reads=[self.r_vfm], writes=[self.r_mod])
            kb.barrier()

    def mcol(self, l, k, c, j):
        return self.modfm[:, l, k, c, j:j + 1]

    def norm_tiles(self, xsrc, l, kg, ksh, tiles, consume, halo=0, cmask=None, out_dt=F32, maxn=512):
        nc, kb = _B(self)
        xv = dram_fm(xsrc)
        W = maxn + 2 * halo
        with self.sbt("n_x", [128, 2, 16, W], F32) as xt, self.sbt("n_sq", [128, 16, W], BF16) as sq, \
                self.sbt("n_h", [128, 2, 16, W], out_dt) as ht, self.sbt("n_r", [128, 2, W], F32) as rs, \
                self.sbt("n_ones", [128, 128], BF16) as ones, self.pst("n_ps", [128, 2, 512], F32) as ps, \
                self.pst("n_ps2", [128, 2, 8], F32) as ps2:
            rx, rh, rr, rp = [Res(), Res()], [Res(), Res()], [Res(), Res()], [Res(), Res()]
            rsq, rones = Res(), Res()
            kb.op("dve", lambda e: e.memset(ones[:], 1.0), writes=[rones])
            for ti, (pc0, n) in enumerate(tiles):
                b = ti % 2
                jj = 1 if pc0 < 258 else 0
                w = n + 2 * halo
                kb.dma("sp", xt[:, b, :, 0:w], xv[:, :, pc0 - halo:pc0 - halo + w], writes=[rx[b]])
                kb.op("act", lambda e: e.activation(out=sq[:, :, 0:w], in_=xt[:, b, :, 0:w], func=AF.Square),
                      reads=[rx[b]], writes=[rsq])
                parts = [(ps[:, b, 0:n], halo, n)]
                if halo:
                    parts += [(ps2[:, b, 0:1], 0, 1), (ps2[:, b, 1:2], w - 1, 1)]
                for (pp, c0, cn) in parts:
                    for c in range(16):
                        kb.op("pe", lambda e: e.matmul(pp, ones[:], sq[:, c, c0:c0 + cn], start=(c == 0), stop=(c == 15)),
                              reads=[rsq, rones], writes=[rp[b]])
                for (pp, c0, cn) in parts:
                    kb.op("act", lambda e: e.activation(out=rs[:, b, c0:c0 + cn], in_=pp, func=AF.Sqrt,
                                                        scale=1.0 / D, bias=1e-6),
                          reads=[rp[b]], writes=[rr[b]])
                kb.op("dve", lambda e: e.reciprocal(rs[:, b, 0:w], rs[:, b, 0:w]), writes=[rr[b]])
                if cmask is not None:
                    kb.op("dve", lambda e: e.tensor_tensor(rs[:, b, 0:w], rs[:, b, 0:w],
                                                           cmask[0][:, pc0 - halo:pc0 - halo + w], ALU.mult),
                          reads=[cmask[1]], writes=[rr[b]])
                for c in range(16):
                    eng = "dve"
                    kb.op(eng, lambda e: e.scalar_tensor_tensor(out=xt[:, b, c, 0:w], in0=xt[:, b, c, 0:w],
                                                                scalar=self.mcol(l, kg, c, jj), in1=rs[:, b, 0:w],
                                                                op0=ALU.mult, op1=ALU.mult),
                          reads=[rr[b], self.r_mod], writes=[rx[b]])
                    if cmask is None:
                        kb.op("act", lambda e: e.activation(out=ht[:, b, c, 0:w], in_=xt[:, b, c, 0:w], func=AF.Identity,
                                                            bias=self.mcol(l, ksh, c, jj), scale=1.0),
                              reads=[rx[b], self.r_mod], writes=[rh[b]])
                    else:
                        kb.op(eng, lambda e: e.scalar_tensor_tensor(out=ht[:, b, c, 0:w],
                                                                    in0=cmask[0][:, pc0 - halo:pc0 - halo + w],
                                                                    scalar=self.mcol(l, ksh, c, jj), in1=xt[:, b, c, 0:w],
                                                                    op0=ALU.mult, op1=ALU.add),
                              reads=[rx[b], self.r_mod, cmask[1]], writes=[rh[b]])
                consume(ti, pc0, n, ht[:, b, :, 0:w], rh[b])
            kb.barrier()


TT256 = [(1, 256)] + [(259 + 256 * j, 256) for j in range(8)]


class Builder3(Builder2):
    def make_mixer(self, xmix, vi, pool):
        nc, kb = _B(self)
        tmp, xo = pool["tmp"], pool["xo"]
        rt, ro = Res(), [Res(), Res()]
        cnt = [0]

        def consume(ti, pc0, n, h, rh):
            hm = h[:, :, 1:n + 1]
            kb.op("pool", lambda e: e.tensor_tensor(tmp[:, :, 0:n], h[:, :, 0:n], h[:, :, 2:n + 2], ALU.add),
                  reads=[rh], writes=[rt])
            kb.op("dve", lambda e: e.scalar_tensor_tensor(out=tmp[:, :, 0:n], in0=tmp[:, :, 0:n], scalar=0.5, in1=hm,
                                                          op0=ALU.mult, op1=ALU.subtract), reads=[rh], writes=[rt])
            for k in range(6):
                b = cnt[0] % 2
                cnt[0] += 1
                for c in range(16):
                    kb.op("dve", lambda e: e.scalar_tensor_tensor(out=xo[:, b, c, 0:n], in0=tmp[:, c, 0:n],
                                                                  scalar=self.vcol(vi[f"mu{k}"], c), in1=hm[:, c, :],
                                                                  op0=ALU.mult, op1=ALU.add),
                          reads=[rt, rh, self.r_vfm], writes=[ro[b]])
                kb.dma("sp", dram_fm(xmix[k])[:, :, pc0:pc0 + n], xo[:, b, :, 0:n], reads=[ro[b]])
        return consume

    def gemm(self, xsrc, K, jobs, tiles, x_stationary=False, wcols=512, tag="g", npsum=6):
        nc, kb = _B(self)
        kp = min(K, 128)
        KC = (K + 127) // 128
        lo = min(t[0] for t in tiles)
        hi = max(t[0] + t[1] for t in tiles)
        span = hi - lo
        with self.sbt(tag + "_x", [kp, KC, span], BF16) as XT, self.sbt(tag + "_w", [kp, NWB, KC, wcols], BF16) as WT, \
                self.pst(tag + "_ps", [128, npsum, 512], F32) as PS:
            rx = Res()
            rw = [Res() for _ in range(NWB)]
            rp = [Res() for _ in range(npsum)]
            xv = xsrc.rearrange("(kc p) n -> p kc n", p=kp)
            step = max(1, KC // 4)
            for k0 in range(0, KC, step):
                kb.dma("sp", XT[:, k0:k0 + step, :], xv[:, k0:k0 + step, lo:hi], writes=[rx])
            wi = 0
            pi = 0
            for (W, E, epi) in jobs:
                wv = W.rearrange("(kc p) n -> p kc n", p=kp)
                for g0 in range(0, E, wcols):
                    wc = min(wcols, E - g0)
                    b = wi % NWB
                    wi += 1
                    kb.dma("pool", WT[:, b, :, 0:wc], wv[:, :, g0:g0 + wc], writes=[rw[b]])
                    for (pc0, n) in tiles:
                        if not x_stationary:
                            for j0 in range(0, wc, 128):
                                m = min(128, wc - j0)
                                q = pi % npsum
                                pi += 1
                                for kc in range(KC):
                                    kb.op("pe", lambda e: e.matmul(PS[0:m, q, 0:n], WT[:, b, kc, j0:j0 + m],
                                                                   XT[:, kc, pc0 - lo:pc0 - lo + n],
                                                                   start=(kc == 0), stop=(kc == KC - 1)),
                                          reads=[rw[b], rx], writes=[rp[q]])
                                epi(g0 + j0, m, pc0, n, PS[0:m, q, 0:n], rp[q])
                        else:
                            for t0 in range(0, n, 128):
                                q = pi % npsum
                                pi += 1
                                for kc in range(KC):
                                    kb.op("pe", lambda e: e.matmul(PS[:, q, 0:wc], XT[:, kc, pc0 - lo + t0:pc0 - lo + t0 + 128],
                                                                   WT[:, b, kc, 0:wc], start=(kc == 0), stop=(kc == KC - 1)),
                                          reads=[rw[b], rx], writes=[rp[q]])
                                epi(g0, wc, pc0 + t0, 128, PS[:, q, 0:wc], rp[q])
            kb.barrier()

    def epi_store(self, dst, stage, func=AF.Identity, bias=None, scale=1.0, post=None):
        nc, kb = _B(self)
        st, rs = stage
        cnt = [0]

        def epi(oc0, m, pc0, n, ps, rps):
            b = cnt[0] % len(rs)
            cnt[0] += 1
            bb = bias(oc0 // 128)[0:m, :] if bias is not None else 0.0
            rd = [rps] + ([self.r_vfm, self.r_mod] if bias is not None else [])
            kb.op("act", lambda e: e.activation(out=st[0:m, b, 0:n], in_=ps, func=func, bias=bb, scale=scale),
                  reads=rd, writes=[rs[b]])
            if post is not None:
                post(oc0, m, pc0, n, st[0:m, b, 0:n], rs[b])
            kb.dma("sp", dst[oc0:oc0 + m, pc0:pc0 + n], st[0:m, b, 0:n], reads=[rs[b]])
        return epi


CH_COLS = [1, 129] + [259 + 128 * j for j in range(16)]


class Builder4(Builder3):
    def scan_pass(self, slot, src, vi, consts, yT, bonusT=None, state_out=None, state_in=None):
        nc, kb = _B(self)
        masks, rmask, bones, ident2 = consts["masks"], consts["rmask"], consts["bones"], consts["ident2"]
        rc = consts["rc"]
        m_st = masks[:, 0] if slot == 0 else masks[:, 1]
        m_ts = masks[:, 1] if slot == 0 else masks[:, 0]
        m_si = masks[:, 2] if slot == 0 else masks[:, 3]
        fr = ["r", "k", "v", "ld", "al"] + (["alo"] if slot == 0 else [])
        sfx = f"s{slot}"
        from contextlib import ExitStack
        with ExitStack() as es:
            def sb(name, shape, dt):
                return es.enter_context(self.sbt(sfx + name, shape, dt))
            def pst(name, shape):
                return es.enter_context(self.pst(sfx + name, shape, F32))
            row = {n: sb("r_" + n, [128, NTP], F32) for n in fr if n != "alo"}
            rr = {n: Res() for n in fr if n != "alo"}
            kkn, W1, W2, W3 = (sb(n, [128, NTP], F32) for n in ("kkn", "W1", "W2", "W3"))
            ks = sb("ks", [128, NTP], F32)
            yrow = sb("T1", [128, NTP], F32)
            At, Bt, Kt, Rt, Bb, Kb_, Vb = (sb(n, [128, NTP], BF16) for n in ("At", "Bt", "Kt", "Rt", "Bb", "Kb", "Vb"))
            ones = sb("ones", [128, 128], F32)
            lmask = consts["lmask"]
            lm_ts = lmask[:, 0] if slot == 0 else lmask[:, 1]
            lm_st = lmask[:, 1] if slot == 0 else lmask[:, 0]
            id2h = consts["id2h"]
            Hf = sb("Hf", [64, 2, 64], F32)
            Hb = sb("Hb", [64, 2, 64], BF16)
            ych = sb("ych", [64, 2, 2, 128], F32)
            rych = [Res(), Res()]
            PG = pst("PG", [128, 2, 512])
            PD = pst("PD", [128, 3, 512])
            PH = pst("PH", [128, 2, 512])
            PHh = pst("PHh", [128, 1, 512])
            rPHh = Res()
            rrow = {n: Res() for n in ("kkn", "W1", "W2", "W3", "ks", "y", "At", "Bt", "Kt", "Rt", "Bb", "Kb", "Vb")}
            if slot == 0:
                row["alo"] = yrow
                rr["alo"] = rrow["y"]
            r1 = Res()
            rHf, rHb = Res(), Res()

            def chunk_set(i):
                t = {}
                for n, shp, dt_ in (("AkT", [128, 2, 128], BF16), ("ArbT", [128, 2, 128], BF16), ("ArkT", [128, 2, 128], BF16),
                                    ("TM", [128, 4, 128], BF16), ("Xf", [128, 2, 128], BF16), ("Xb", [128, 2, 128], BF16),
                                    ("No", [128, 7, 256], BF16), ("NoT", [128, 7, 256], BF16), ("Tm", [128, 2, 256], BF16),
                                    ("TTm", [128, 2, 256], BF16), ("A1", [128, 256], BF16), ("B1", [128, 256], BF16),
                                    ("MTs", [64, 2, 64], BF16), ("pC2", [64, 2], F32), ("Rh", [64, 2, 128], BF16),
                                    ("Amk", [128, 2, 128], BF16), ("Bmk", [128, 2, 128], BF16), ("Kmk", [128, 2, 128], BF16)):
                    t[n] = sb(f"{n}{i}", shp, dt_)
                for n in ("rAk", "rArb", "rArk", "rTM", "rXf", "rXb", "rMT", "rRh", "rNo", "rNoT", "rA1", "rB1", "rMk"):
                    t[n] = Res()
                t["rT"] = [Res(), Res()]
                t["rTT"] = [Res(), Res()]
                return t
            NCS = 3
            csets = [chunk_set(i) for i in range(NCS)]
            rPG = [Res() for _ in range(2)]
            rPD = [Res() for _ in range(3)]
            rPH = [Res(), Res()]
            gi = [0]
            di = [0]

            def pg():
                i = gi[0] % 2
                gi[0] += 1
                return PG[:, i, 0:256], rPG[i]

            def pd():
                i = di[0] % 3
                di[0] += 1
                return PD[:, i, 0:256], rPD[i]

            kb.op("dve", lambda e: e.memset(ones[:], 1.0), writes=[r1])
            kb.op("dve", lambda e: e.memset(W3[:], 0.0), writes=[rrow["W3"]])
            V = lambda nm, c: self.vcol(vi[nm], c)
            for c in range(getattr(self, 'scan_nc', 16)):
                for n in fr:
                    kb.dma("sp", row[n][:], src[n][c * 128:(c + 1) * 128, :], writes=[rr[n]])
                r_, k_, v_, ld, al = (row[n] for n in ("r", "k", "v", "ld", "al"))
                kb.op("dve", lambda e: e.tensor_scalar(kkn[:], k_[:], V("kk", c), None, ALU.mult),
                      reads=[rr["k"], self.r_vfm], writes=[rrow["kkn"]])
                kb.op("act", lambda e: e.activation(out=W1[:], in_=kkn[:], func=AF.Square),
                      reads=[rrow["kkn"]], writes=[rrow["W1"]])
                for p0 in range(0, NTP, 256):
                    n = min(256, NTP - p0)
                    ps, rps = pg()
                    kb.op("pe", lambda e: e.matmul(ps[:, 0:n], bones, W1[:, p0:p0 + n], start=True, stop=True),
                          reads=[rrow["W1"], rc], writes=[rps])
                    kb.op("dve", lambda e: e.tensor_scalar(W2[:, p0:p0 + n], ps[:, 0:n], 1e-24, None, ALU.max),
                          reads=[rps], writes=[rrow["W2"]])
                kb.op("act", lambda e: e.activation(out=W2[:], in_=W2[:], func=AF.Sqrt), writes=[rrow["W2"]])
                kb.op("dve", lambda e: e.reciprocal(W2[:], W2[:]), writes=[rrow["W2"]])
                kb.op("dve", lambda e: e.tensor_tensor(kkn[:], kkn[:], W2[:], ALU.mult),
                      reads=[rrow["W2"]], writes=[rrow["kkn"]])
                kb.op("dve", lambda e: e.tensor_scalar(ks[:], al[:], -1.0, V("ka", c), ALU.add, ALU.mult),
                      reads=[rr["al"], self.r_vfm], writes=[rrow["ks"]])
                kb.op("dve", lambda e: e.scalar_tensor_tensor(out=ks[:], in0=ks[:], scalar=1.0, in1=k_[:],
                                                              op0=ALU.add, op1=ALU.mult),
                      reads=[rr["k"]], writes=[rrow["ks"]])
                if slot == 0:
                    alo = row["alo"]
                    kb.op("dve", lambda e: e.tensor_scalar(W1[:], alo[:], -1.0, V("ka", c), ALU.add, ALU.mult),
                          reads=[rr["alo"], self.r_vfm], writes=[rrow["W1"]])
                    kb.op("dve", lambda e: e.scalar_tensor_tensor(out=W1[:], in0=W1[:], scalar=1.0, in1=k_[:],
                                                                  op0=ALU.add, op1=ALU.mult),
                          reads=[rr["k"]], writes=[rrow["W1"]])
                    kb.op("pool", lambda e: e.tensor_tensor(W1[:], W1[:], ks[:], ALU.add),
                          reads=[rrow["ks"]], writes=[rrow["W1"]])
                    kb.op("dve", lambda e: e.scalar_tensor_tensor(out=W1[:], in0=r_[:], scalar=V("rk", c), in1=W1[:],
                                                                  op0=ALU.mult, op1=ALU.mult),
                          reads=[rr["r"], self.r_vfm], writes=[rrow["W1"]])
                    for p0 in range(0, NTP, 256):
                        n = min(256, NTP - p0)
                        ps, rps = pg()
                        kb.op("pe", lambda e: e.matmul(ps[:, 0:n], bones, W1[:, p0:p0 + n], start=True, stop=True),
                              reads=[rrow["W1"], rc], writes=[rps])
                        kb.op("dve", lambda e: e.tensor_tensor(W3[:, p0:p0 + n], ps[:, 0:n], v_[:, p0:p0 + n], ALU.mult),
                              reads=[rps, rr["v"]], writes=[rrow["W3"]])
                    kb.dma("sp", bonusT[c * 128:(c + 1) * 128, :], W3[:], reads=[rrow["W3"]])
                import os
                for cj in CH_COLS:
                    if os.environ.get('NO_SCAN'):
                        kb.op("dve", lambda e: e.tensor_copy(W1[:, cj:cj + 128], ld[:, cj:cj + 128]),
                              reads=[rr["ld"], r1], writes=[rrow["W1"]])
                        continue
                    kb.op("dve", lambda e: e.tensor_tensor_scan(W1[:, cj:cj + 128], ones[:], ld[:, cj:cj + 128], 0.0,
                                                                ALU.mult, ALU.add),
                          reads=[rr["ld"], r1], writes=[rrow["W1"]])
                if slot == 0:
                    L, rL = W1, rrow["W1"]
                else:
                    kb.op("pool", lambda e: e.tensor_tensor(W2[:], ld[:], W1[:], ALU.subtract),
                          reads=[rr["ld"], rrow["W1"]], writes=[rrow["W2"]])
                    for cj in CH_COLS:
                        kb.op("dve", lambda e: e.tensor_scalar(W3[:, cj:cj + 128], W2[:, cj:cj + 128],
                                                               W1[:, cj + 127:cj + 128], None, ALU.add),
                              reads=[rrow["W2"], rrow["W1"]], writes=[rrow["W3"]])
                    L, rL = W3, rrow["W3"]
                kb.op("pool", lambda e: e.tensor_tensor(W2[:], L[:], ld[:], ALU.subtract),
                      reads=[rL, rr["ld"]], writes=[rrow["W2"]])
                kb.op("act", lambda e: e.activation(out=W2[:], in_=W2[:], func=AF.Exp), writes=[rrow["W2"]])
                kb.op("act", lambda e: e.activation(out=yrow[:], in_=L[:], func=AF.Exp, scale=-1.0),
                      reads=[rL], writes=[rrow["y"]])
                kb.op("act", lambda e: e.activation(out=L[:], in_=L[:], func=AF.Exp), writes=[rL])
                kb.op("dve", lambda e: e.scalar_tensor_tensor(out=At[:], in0=kkn[:], scalar=-1.0, in1=W2[:],
                                                              op0=ALU.mult, op1=ALU.mult),
                      reads=[rrow["kkn"], rrow["W2"]], writes=[rrow["At"]])
                kb.op("pool", lambda e: e.tensor_tensor(kkn[:], kkn[:], al[:], ALU.mult),
                      reads=[rr["al"], rrow["At"]], writes=[rrow["kkn"]])
                kb.op("dve", lambda e: e.tensor_tensor(Bt[:], kkn[:], yrow[:], ALU.mult),
                      reads=[rrow["kkn"], rrow["y"]], writes=[rrow["Bt"]])
                kb.op("pool", lambda e: e.tensor_tensor(Kt[:], ks[:], yrow[:], ALU.mult),
                      reads=[rrow["ks"], rrow["y"]], writes=[rrow["Kt"]])
                kb.op("dve", lambda e: e.tensor_tensor(Rt[:], r_[:], L[:], ALU.mult),
                      reads=[rr["r"], rL], writes=[rrow["Rt"]])
                kb.op("act", lambda e: e.copy(Vb[:], v_[:]), reads=[rr["v"]], writes=[rrow["Vb"]])
                for cj in CH_COLS:
                    pcx = cj + 127 if slot == 0 else cj
                    kb.op("dve", lambda e: e.tensor_scalar(Bb[:, cj:cj + 128], Bt[:, cj:cj + 128], L[:, pcx:pcx + 1], None, ALU.mult),
                          reads=[rrow["Bt"], rL], writes=[rrow["Bb"]])
                    kb.op("pool", lambda e: e.tensor_scalar(Kb_[:, cj:cj + 128], Kt[:, cj:cj + 128], L[:, pcx:pcx + 1], None, ALU.mult),
                          reads=[rrow["Kt"], rL], writes=[rrow["Kb"]])
                import os
                stop = int(os.environ.get('SCAN_STOP', '99'))
                order = list(range(18)) if slot == 0 else [1, 0] + list(range(17, 1, -1))
                kb.op("dve", lambda e: e.memset(Hf[:], 0.0), writes=[rHf])
                kb.op("dve", lambda e: e.memset(Hb[:], 0.0), writes=[rHb])
                HS = [slice(0, 64), slice(64, 128)]
                def chunk_pre(oi, j):
                    cj = CH_COLS[j]
                    cs = slice(cj, cj + 128)
                    T_ = csets[oi % NCS]
                    AkT, ArbT, ArkT, TM, Xf, Xb, No, NoT, Tm, TTm, A1, B1, MTs, pC2, Rh = (T_[n] for n in (
                        "AkT", "ArbT", "ArkT", "TM", "Xf", "Xb", "No", "NoT", "Tm", "TTm", "A1", "B1", "MTs", "pC2", "Rh"))
                    rAk, rArb, rArk, rTM, rXf, rXb, rMT, rRh, rNo, rNoT, rA1, rB1, rT, rTT = (T_[n] for n in (
                        "rAk", "rArb", "rArk", "rTM", "rXf", "rXb", "rMT", "rRh", "rNo", "rNoT", "rA1", "rB1", "rT", "rTT"))

                    def gram(lh, rlh, rh, rrh, mask, dst, rdst):
                        ps, rps = pg()
                        for hh in range(2):
                            kb.op("pe", lambda e: e.matmul(ps[:, hh * 128:(hh + 1) * 128], lh[hh], rh[:, cs],
                                                           start=True, stop=True),
                                  reads=[rlh[hh], rrh], writes=[rps])
                        kb.op("dve", lambda e: e.tensor_tensor(dst, ps[:], mask, ALU.mult), reads=[rps, rc], writes=[rdst])
                    Amk, Bmk, Kmk, rMk = T_["Amk"], T_["Bmk"], T_["Kmk"], T_["rMk"]
                    for hh in range(2):
                        hm = bones[:, hh * 64:hh * 64 + 1]
                        for (srcr, rs_, dstl) in ((At, rrow["At"], Amk), (Bt, rrow["Bt"], Bmk), (Kt, rrow["Kt"], Kmk)):
                            kb.op("dve", lambda e: e.tensor_scalar(dstl[:, hh, :], srcr[:, cs], hm, None, ALU.mult),
                                  reads=[rs_, rc], writes=[rMk])
                    Ath = [Amk[:, 0, :], Amk[:, 1, :]]
                    Bth = [Bmk[:, 0, :], Bmk[:, 1, :]]
                    Kth = [Kmk[:, 0, :], Kmk[:, 1, :]]
                    rAth = rBth = rKth = [rMk, rMk]
                    for (lh, rlh, rh, rrh, lm, dst, rdst) in ((Bth, rBth, At, rrow["At"], lm_st, NoT, rNoT), (Ath, rAth, Bt, rrow["Bt"], lm_ts, No, rNo)):
                        ps, rps = pg()
                        for hh in range(2):
                            kb.op("pe", lambda e: e.matmul(ps[:, hh * 128:(hh + 1) * 128], lh[hh], rh[:, cs], start=True, stop=True),
                                  reads=[rlh[hh], rrh], writes=[rps])
                        for l in range(7):
                            kb.op("dve", lambda e: e.tensor_tensor(dst[:, l, :], ps[:], lm[:, l, :], ALU.mult), reads=[rps, rc], writes=[rdst])
                    gram(Kth, rKth, At, rrow["At"], m_st, AkT[:].rearrange("p h t -> p (h t)"), rAk)
                    gram(Bth, rBth, Rt, rrow["Rt"], m_si, ArbT[:].rearrange("p h t -> p (h t)"), rArb)
                    gram(Kth, rKth, Rt, rrow["Rt"], m_si, ArkT[:].rearrange("p h t -> p (h t)"), rArk)
                    yield
                    for q2, (srcr, rsrc) in enumerate(((At, rrow["At"]), (Bb, rrow["Bb"]), (Kb_, rrow["Kb"]), (Vb, rrow["Vb"]))):
                        if q2 % 2 == 0:
                            ps, rps = pg()
                        kb.op("pe", lambda e: e.matmul(ps[:, (q2 % 2) * 128:(q2 % 2 + 1) * 128], srcr[:, cs], self.ident_b,
                                                       start=True, stop=True),
                              reads=[rsrc, self.r_ident], writes=[rps])
                        if q2 % 2 == 1:
                            kb.op("act", lambda e: e.copy(TM[:, q2 - 1:q2 + 1, :].rearrange("p a t -> p (a t)"), ps[:]),
                                  reads=[rps], writes=[rTM])
                    ps, rps = pg()
                    for hh in range(2):
                        kb.op("pe", lambda e: e.matmul(ps[:, hh * 64:(hh + 1) * 64], AkT[:, hh, :], TM[:, 3, HS[hh]],
                                                       start=True, stop=True),
                              reads=[rAk, rTM], writes=[rps])
                    for hh in range(2):
                        kb.op("dve", lambda e: e.tensor_copy(Xb[:, hh, 0:64], TM[:, 0, HS[hh]]), reads=[rTM], writes=[rXb])
                        kb.op("dve", lambda e: e.tensor_copy(Xb[:, hh, 64:128], ps[:, hh * 64:(hh + 1) * 64]),
                              reads=[rps], writes=[rXb])
                    yield
                    kb.op("dve", lambda e: e.tensor_tensor(Tm[:, 0, :], No[:, 0, :], id2h, ALU.add), reads=[rNo, rc], writes=[rT[0]])
                    kb.op("dve", lambda e: e.tensor_tensor(TTm[:, 0, :], NoT[:, 0, :], id2h, ALU.add), reads=[rNoT, rc], writes=[rTT[0]])
                    for l in range(1, 7):
                        a, b2 = (l - 1) % 2, l % 2
                        psA, rpsA = pd()
                        for hh in range(2):
                            kb.op("pe", lambda e: e.matmul(psA[:, hh * 128:(hh + 1) * 128], NoT[:, l, hh * 128:(hh + 1) * 128],
                                                           Tm[:, a, hh * 128:(hh + 1) * 128], start=True, stop=True),
                                  reads=[rNoT, rT[a]], writes=[rpsA])
                        kb.op("act", lambda e: e.copy(A1[:], psA[:]), reads=[rpsA], writes=[rA1])
                        psB, rpsB = pd()
                        for hh in range(2):
                            kb.op("pe", lambda e: e.matmul(psB[:, hh * 128:(hh + 1) * 128], No[:, l, hh * 128:(hh + 1) * 128],
                                                           TTm[:, a, hh * 128:(hh + 1) * 128], start=True, stop=True),
                                  reads=[rNo, rTT[a]], writes=[rpsB])
                        kb.op("dve", lambda e: e.tensor_copy(B1[:], psB[:]), reads=[rpsB], writes=[rB1])
                        yield
                        if l < 6:
                            psT, rpsT = pd()
                            for hh in range(2):
                                o_ = psT[:, hh * 128:(hh + 1) * 128]
                                kb.op("pe", lambda e: e.matmul(o_, TTm[:, a, hh * 128:(hh + 1) * 128], A1[:, hh * 128:(hh + 1) * 128],
                                                               start=True, stop=True), reads=[rTT[a], rA1], writes=[rpsT])
                            kb.op("dve", lambda e: e.tensor_tensor(Tm[:, b2, :], psT[:], Tm[:, a, :], ALU.add),
                                  reads=[rpsT, rT[a]], writes=[rT[b2]])
                        psU, rpsU = pd()
                        for hh in range(2):
                            o_ = psU[:, hh * 128:(hh + 1) * 128]
                            kb.op("pe", lambda e: e.matmul(o_, Tm[:, a, hh * 128:(hh + 1) * 128], B1[:, hh * 128:(hh + 1) * 128],
                                                           start=True, stop=True), reads=[rT[a], rB1], writes=[rpsU])
                        kb.op("dve", lambda e: e.tensor_tensor(TTm[:, b2, :], psU[:], TTm[:, a, :], ALU.add),
                              reads=[rpsU, rTT[a]], writes=[rTT[b2]])
                        yield
                    psX, rpsX = pd()
                    for hh in range(2):
                        kb.op("pe", lambda e: e.matmul(psX[:, hh * 128:(hh + 1) * 128], TTm[:, 0, hh * 128:(hh + 1) * 128], Xb[:, hh, :],
                                                       start=True, stop=True), reads=[rTT[0], rXb], writes=[rpsX])
                    kb.op("act", lambda e: e.copy(Xf[:].rearrange("p h t -> p (h t)"), psX[:]), reads=[rpsX], writes=[rXf])
                    yield
                    psr, rpsr = pg()
                    for hh in range(2):
                        kb.op("pe", lambda e: e.matmul(psr[0:64, hh * 128:(hh + 1) * 128], Xf[:, hh, 0:64], ArbT[:, hh, :],
                                                       start=True, stop=False),
                              reads=[rXf, rArb], writes=[rpsr])
                        kb.op("pe", lambda e: e.matmul(psr[0:64, hh * 128:(hh + 1) * 128], self.ident_b[:, HS[hh]], Rt[:, cs],
                                                       start=False, stop=True),
                              reads=[rrow["Rt"], self.r_ident], writes=[rpsr])
                    kb.op("act", lambda e: e.copy(Rh[:].rearrange("p h t -> p (h t)"), psr[0:64, :]), reads=[rpsr], writes=[rRh])
                    psm, rpsm = pg()
                    pcx = cj + 127 if slot == 0 else cj
                    for hh in range(2):
                        kb.op("pe", lambda e: e.matmul(psm[0:64, hh * 64:(hh + 1) * 64], Xf[:, hh, 0:64], TM[:, 1, HS[hh]],
                                                       start=True, stop=True),
                              reads=[rXf, rTM], writes=[rpsm])
                        kb.op("pe", lambda e: e.matmul(psm[0:64, 128 + hh:129 + hh], self.ident_f[:, HS[hh]], L[:, pcx:pcx + 1],
                                                       start=True, stop=True),
                              reads=[rL, self.r_ident], writes=[rpsm])
                    kb.op("dve", lambda e: e.tensor_copy(MTs[:].rearrange("p h t -> p (h t)"), psm[0:64, 0:128]),
                          reads=[rpsm], writes=[rMT])
                    kb.op("dve", lambda e: e.tensor_copy(pC2[:], psm[0:64, 128:130]), reads=[rpsm], writes=[rMT])
                    yield

                def chunk_tail(oi, j):
                    cj = CH_COLS[j]
                    cs = slice(cj, cj + 128)
                    T_ = csets[oi % NCS]
                    AkT, ArbT, ArkT, TM, Xf, Xb, No, NoT, Tm, TTm, A1, B1, MTs, pC2, Rh = (T_[n] for n in (
                        "AkT", "ArbT", "ArkT", "TM", "Xf", "Xb", "No", "NoT", "Tm", "TTm", "A1", "B1", "MTs", "pC2", "Rh"))
                    rAk, rArb, rArk, rTM, rXf, rXb, rMT, rRh, rNo, rNoT, rA1, rB1, rT, rTT = (T_[n] for n in (
                        "rAk", "rArb", "rArk", "rTM", "rXf", "rXb", "rMT", "rRh", "rNo", "rNoT", "rA1", "rB1", "rT", "rTT"))
                    if slot == 1 and oi == 2:
                        kb.dma("sp", Hf[:], state_in[c * 128:(c + 1) * 128, :].rearrange("(hh k) v -> k hh v", hh=2), writes=[rHf])
                        kb.op("act", lambda e: e.copy(Hb[:], Hf[:]), reads=[rHf], writes=[rHb])
                    b = oi % 2
                    psy = PH[0:64, b, 0:256]
                    psh = PHh[0:64, 0, 0:128]
                    yh_mode = os.environ.get('SCAN_YH', 'yh')
                    for hh in range(2 if 'y' in yh_mode else 0):
                        yo = psy[:, hh * 128:(hh + 1) * 128]
                        kb.op("pe", lambda e: e.matmul(yo, Hb[:, hh, :], Rh[:, hh, :], start=True, stop=False),
                              reads=[rHb, rRh], writes=[rPH[b]])
                        kb.op("pe", lambda e: e.matmul(yo, Xf[:, hh, 64:128], ArbT[:, hh, :], start=False, stop=False),
                              reads=[rXf, rArb], writes=[rPH[b]])
                        kb.op("pe", lambda e: e.matmul(yo, TM[:, 3, HS[hh]], ArkT[:, hh, :], start=False, stop=True),
                              reads=[rTM, rArk], writes=[rPH[b]])
                    for hh in range(2 if 'h' in yh_mode else 0):
                        ho = psh[:, hh * 64:(hh + 1) * 64]
                        kb.op("pe", lambda e: e.matmul(ho, MTs[:, hh, :], Hb[:, hh, :], start=True, stop=False),
                              reads=[rMT, rHb], writes=[rPHh])
                        kb.op("pe", lambda e: e.matmul(ho, TM[:, 1, HS[hh]], Xf[:, hh, 64:128], start=False, stop=False),
                              reads=[rTM, rXf], writes=[rPHh])
                        kb.op("pe", lambda e: e.matmul(ho, TM[:, 2, HS[hh]], TM[:, 3, HS[hh]], start=False, stop=True),
                              reads=[rTM], writes=[rPHh])
                    if 'y' in yh_mode:
                        kb.op("act", lambda e: e.copy(ych[:, b].rearrange("p h t -> p (h t)"), psy), reads=[rPH[b]], writes=[rych[b]])
                    if 'd' in yh_mode or yh_mode == 'yh':
                      kb.dma("sp", yT[c * 128:(c + 1) * 128, cs].rearrange("(hh v) t -> v hh t", hh=2), ych[:, b], reads=[rych[b]])
                    for hh in range(2 if 'h' in yh_mode else 0):
                        kb.op("dve", lambda e: e.scalar_tensor_tensor(out=Hf[:, hh, :], in0=Hf[:, hh, :], scalar=pC2[:, hh:hh + 1],
                                                                      in1=psh[:, hh * 64:(hh + 1) * 64], op0=ALU.mult, op1=ALU.add),
                              reads=[rPHh, rMT], writes=[rHf])
                    kb.op("act", lambda e: e.copy(Hb[:], Hf[:]), reads=[rHf], writes=[rHb])

                if stop > 5:
                    for p0 in range(0, len(order), NCS):
                        grp = [(oi, order[oi]) for oi in range(p0, min(p0 + NCS, len(order)))]
                        gens = [chunk_pre(oi, j) for (oi, j) in grp]
                        live = list(gens)
                        while live:
                            for g_ in list(live):
                                try:
                                    next(g_)
                                except StopIteration:
                                    live.remove(g_)
                        for (oi, j) in grp:
                            chunk_tail(oi, j)
                if state_out is not None:
                    kb.dma("sp", state_out[c * 128:(c + 1) * 128, :].rearrange("(hh k) v -> k hh v", hh=2), Hf[:], reads=[rHf])
            kb.barrier()


GN_EPS = 64e-5
VEC_NAMES = (["c", "cctx"] + [f"modb{l}_{k}" for l in range(2) for k in range(6)] + ["n1g0", "n2g0", "n1g1", "n2g1"]
             + [f"mu{k}" for k in range(6)] + ["w0A", "w0B", "a0A", "a0B", "kk", "ka", "rk", "lnw", "lnb"])
VI = {n: i for i, n in enumerate(VEC_NAMES)}
NCST = 4 * 256 + 128 + 64 + 256
NLM = 2 * 7 * 256


class Builder5(Builder4):
    def stage(self, es, tag, dt, nb=3, w=512):
        t = es.enter_context(self.sbt(tag, [128, nb, w], dt))
        return t, [Res() for _ in range(nb)]

    def epi_resid(self, xT, l, kgate, st_out, st_in):
        nc, kb = _B(self)
        st, rs = st_out
        xin, rxin = st_in
        cnt = [0]

        def epi(oc0, m, pc0, n, ps, rps):
            b = cnt[0] % len(rs)
            cnt[0] += 1
            jj = 1 if pc0 < 258 else 0
            c = oc0 // 128
            kb.dma("sp", xin[0:m, b, 0:n], xT[oc0:oc0 + m, pc0:pc0 + n], writes=[rxin[b]])
            kb.op("dve", lambda e: e.scalar_tensor_tensor(out=st[0:m, b, 0:n], in0=ps, scalar=self.mcol(l, kgate, c, jj)[0:m, :],
                                                          in1=xin[0:m, b, 0:n], op0=ALU.mult, op1=ALU.add),
                  reads=[rps, rxin[b], self.r_mod], writes=[rs[b]])
            kb.dma("sp", xT[oc0:oc0 + m, pc0:pc0 + n], st[0:m, b, 0:n], reads=[rs[b]])
        return epi

    def epi_mul(self, dst, st_out, st_in):
        nc, kb = _B(self)
        st, rs = st_out
        xin, rxin = st_in
        cnt = [0]

        def epi(oc0, m, pc0, n, ps, rps):
            b = cnt[0] % len(rs)
            cnt[0] += 1
            kb.dma("sp", xin[0:m, b, 0:n], dst[oc0:oc0 + m, pc0:pc0 + n], writes=[rxin[b]])
            kb.op("dve", lambda e: e.tensor_tensor(st[0:m, b, 0:n], ps, xin[0:m, b, 0:n], ALU.mult),
                  reads=[rps, rxin[b]], writes=[rs[b]])
            kb.dma("sp", dst[oc0:oc0 + m, pc0:pc0 + n], st[0:m, b, 0:n], reads=[rs[b]])
        return epi

    def ffn(self, xT, l, wg, wu, wd, h2_d, hid_d, tiles):
        nc, kb = _B(self)
        from contextlib import ExitStack
        hv = dram_fm(h2_d)

        def consume(ti, pc0, n, h, rh):
            kb.dma("sp", hv[:, :, pc0:pc0 + n], h, reads=[rh])
        self.norm_tiles(xT, l, 4, 3, tiles, consume, out_dt=BF16)
        with ExitStack() as es:
            so = self.stage(es, "f_so", BF16)
            si = self.stage(es, "f_si", BF16)
            self.gemm(h2_d, D, [(wg, DFF, self.epi_store(hid_d, so, func=AF.Silu)),
                                (wu, DFF, self.epi_mul(hid_d, so, si))], tiles, tag="fg")
        groups = [tiles[i:i + 2] for i in range(0, len(tiles), 2)]
        for gi, grp in enumerate(groups):
            with ExitStack() as es:
                so = self.stage(es, "d_so", F32, nb=2)
                si = self.stage(es, "d_si", F32, nb=2)
                self.gemm(hid_d, DFF, [(wd, D, self.epi_resid(xT, l, 5, so, si))], grp, wcols=256, tag=f"fd{gi}")

    def rwkv_readout(self, yT0, yT1, bonusT, gT, zT, consts):
        nc, kb = _B(self)
        bones, rc = consts["bones"], consts["rc"]
        with self.sbt("ro_y", [128, NTP], F32) as y, self.sbt("ro_y1", [128, NTP], F32) as y1, \
                self.sbt("ro_b", [128, NTP], F32) as bo, self.sbt("ro_g", [128, NTP], F32) as g, \
                self.sbt("ro_d", [128, NTP], F32) as d, self.sbt("ro_s", [128, NTP], F32) as sq, \
                self.sbt("ro_z", [128, NTP], BF16) as z, self.pst("ro_ps", [128, 4, 512], F32) as PS:
            ry, ry1, rb, rg, rd, rs, rz = (Res() for _ in range(7))
            rp = [Res() for _ in range(4)]
            pi = [0]

            def pp():
                i = pi[0] % 4
                pi[0] += 1
                return PS[:, i, 0:256], rp[i]
            for c in range(16):
                rows = slice(c * 128, (c + 1) * 128)
                kb.dma("sp", y[:], yT0[rows, :], writes=[ry])
                kb.dma("sp", y1[:], yT1[rows, :], writes=[ry1])
                kb.dma("sp", bo[:], bonusT[rows, :], writes=[rb])
                kb.dma("sp", g[:], gT[rows, :], writes=[rg])
                kb.op("dve", lambda e: e.tensor_tensor(y[:], y[:], y1[:], ALU.add), reads=[ry1], writes=[ry])
                for p0 in range(0, NTP, 256):
                    n = min(256, NTP - p0)
                    ps, rps = pp()
                    kb.op("pe", lambda e: e.matmul(ps[:, 0:n], bones, y[:, p0:p0 + n], start=True, stop=True),
                          reads=[ry, rc], writes=[rps])
                    kb.op("dve", lambda e: e.scalar_tensor_tensor(out=d[:, p0:p0 + n], in0=ps[:, 0:n], scalar=-1.0 / 64,
                                                                  in1=y[:, p0:p0 + n], op0=ALU.mult, op1=ALU.add),
                          reads=[rps, ry], writes=[rd])
                kb.op("act", lambda e: e.activation(out=sq[:], in_=d[:], func=AF.Square), reads=[rd], writes=[rs])
                for p0 in range(0, NTP, 256):
                    n = min(256, NTP - p0)
                    ps, rps = pp()
                    kb.op("pe", lambda e: e.matmul(ps[:, 0:n], bones, sq[:, p0:p0 + n], start=True, stop=True),
                          reads=[rs, rc], writes=[rps])
                    kb.op("act", lambda e: e.activation(out=y1[:, p0:p0 + n], in_=ps[:, 0:n], func=AF.Sqrt, scale=1.0 / 64,
                                                        bias=GN_EPS), reads=[rps], writes=[ry1])
                kb.op("dve", lambda e: e.reciprocal(y1[:], y1[:]), writes=[ry1])
                kb.op("dve", lambda e: e.tensor_tensor(d[:], d[:], y1[:], ALU.mult), reads=[ry1], writes=[rd])
                kb.op("dve", lambda e: e.tensor_scalar(d[:], d[:], self.vcol(VI["lnw"], c), self.vcol(VI["lnb"], c),
                                                       ALU.mult, ALU.add), reads=[self.r_vfm], writes=[rd])
                kb.op("dve", lambda e: e.tensor_tensor(d[:], d[:], bo[:], ALU.add), reads=[rb], writes=[rd])
                kb.op("dve", lambda e: e.tensor_tensor(z[:], d[:], g[:], ALU.mult), reads=[rd, rg], writes=[rz])
                kb.dma("sp", zT[rows, :], z[:], reads=[rz])
            kb.barrier()

    def exchange_state(self, stateA, gath, state_in, sel_d):
        nc, kb = _B(self)
        kb.barrier()
        ccsem = kb._newsem("cc")
        with nc.Block() as blk:
            @blk.gpsimd
            def _(g):
                g.collective_compute("AllGather", ALU.bypass, replica_groups=[[0, 1], [2, 3], [4, 5], [6, 7]],
                                     ins=[stateA[:, :]], outs=[gath[:, :]]).then_inc(ccsem, 1)
                g.wait_ge(ccsem, 1)
        kb.waited["pool"][ccsem.name] = 1
        rg = Res()
        rg.w = {ccsem.name: (ccsem, 1, "cc")}
        with self.sbt("x_g", [128, 2, 16, 64], F32) as gt, self.sbt("x_s", [128, 2], F32) as sl:
            r1, r2 = Res(), Res()
            for i in range(2):
                kb.dma("sp", gt[:, i], gath[i * D:(i + 1) * D, :].rearrange("(c p) v -> p c v", p=128), reads=[rg], writes=[r1])
            kb.dma("sp", sl[:], sel_d.partition_broadcast(128), writes=[r2])
            kb.op("dve", lambda e: e.tensor_scalar(gt[:, 0], gt[:, 0], sl[:, 0:1], None, ALU.mult), reads=[r2], writes=[r1])
            kb.op("dve", lambda e: e.scalar_tensor_tensor(out=gt[:, 0], in0=gt[:, 1], scalar=sl[:, 1:2], in1=gt[:, 0],
                                                          op0=ALU.mult, op1=ALU.add), reads=[r2], writes=[r1])
            kb.dma("sp", state_in.rearrange("(c p) v -> p c v", p=128), gt[:, 0], reads=[r1])
            kb.barrier()


NEG_E05 = -0.6065306597126334
LAMBDA_INIT1 = 0.8 - 0.6 * 0.7408182206817179


def build_program(debug=(), upto="all"):
    from contextlib import ExitStack
    B = Builder6(debug=list(debug))
    self = B
    nc, kb = B.nc, B.kb
    I = {}
    for n, shp in (("ident", [128, 128]), ("vec_all", [len(VEC_NAMES) * 16, 128]), ("cst", [128, NCST]), ("lmask", [128, NLM]), ("cmask", [1, NTP]),
                   ("sel", [1, 2]), ("xo", [NOWN, D]), ("xc", [NCTX, D]), ("xh", [2, D]), ("mod_w", [2, D, 6 * D]),
                   ("rwkv_wr", [D, D]), ("rwkv_wk", [D, D]), ("rwkv_wv", [D, D]), ("rwkv_wo", [D, D]),
                   ("rwkv_w1", [2, D, LW]), ("rwkv_w2", [2, LW, D]), ("rwkv_a1", [2, D, LW]), ("rwkv_a2", [2, LW, D]),
                   ("rwkv_g1", [D, LG]), ("rwkv_g2", [LG, D]),
                   ("ffn_wg", [2, D, DFF]), ("ffn_wu", [2, D, DFF]), ("ffn_wd", [2, DFF, D])):
        I[n] = B.din(n, shp)
    out = nc.dram_tensor("out", [NOWN, D], F32, kind="ExternalOutput").ap()
    S = {}
    for n in ("xT", "rT", "kT", "vT", "gT", "ld0", "ld1", "al0", "al1", "yT0", "yT1", "bonusT"):
        S[n] = B.dscr(n, [D, NTP])
    for k in range(6):
        S[f"xmix{k}"] = B.dscr(f"xmix{k}", [D, NTP], BF16)
    for s_ in range(2):
        S[f"lw{s_}"] = B.dscr(f"lw{s_}", [LW, NTP], BF16)
        S[f"la{s_}"] = B.dscr(f"la{s_}", [LW, NTP], BF16)
    S["lg"] = B.dscr("lg", [LG, NTP], BF16)
    S["zT"] = B.dscr("zT", [D, NTP], BF16)
    S["h2"] = B.dscr("h2", [D, NTP], BF16)
    S["hid"] = B.dscr("hid", [DFF, NTP], BF16)
    S["stateA"] = B.dscr("stateA", [D, 64])
    S["gath"] = B.dscr("gath", [2 * D, 64])
    S["state_in"] = B.dscr("state_in", [D, 64])

    B.load_consts(I["ident"], None, I["vec_all"], len(VEC_NAMES))
    cs = nc.alloc_sbuf_tensor("cst_sb", [128, NCST], F32).ap()
    rc = Res()
    kb.dma("sp", cs, I["cst"], writes=[rc])
    lmt = nc.alloc_sbuf_tensor("lmask_sb", [128, NLM], BF16).ap()
    kb.dma("pool", lmt, I["lmask"], writes=[rc])
    consts = {"masks": cs[:, 0:1024].rearrange("p (m t) -> p m t", m=4), "rmask": None, "bones": cs[:, 1024:1152],
              "ident2": cs[:, 1152:1216], "id2h": cs[:, 1216:1472], "rc": rc,
              "lmask": lmt.rearrange("p (a l t) -> p a l t", a=2, l=7)}
    xT = S["xT"]
    B.rows_to_fm([(I["xc"], NCTX, 1), (I["xh"][0:1, :], 1, 258), (I["xo"], NOWN, 259), (I["xh"][1:2, :], 1, 2307)], xT)
    B.adaln(I["mod_w"], VI)
    with ExitStack() as es:
        cmt = es.enter_context(self.sbt("cmt", [128, NTP], F32))
        rcm = Res()
        kb.dma("sp", cmt[:], I["cmask"].partition_broadcast(128), writes=[rcm])
        tmp = es.enter_context(self.sbt("mx_tmp", [128, 16, 256], F32))
        xo_t = es.enter_context(self.sbt("mx_xo", [128, 2, 16, 256], BF16))
        mixer = B.make_mixer([S[f"xmix{k}"] for k in range(6)], VI, {"tmp": tmp, "xo": xo_t})
        B.norm_tiles(xT, 0, 1, 0, TT256, mixer, halo=1, cmask=(cmt[:], rcm), maxn=256)
    if upto == "mix":
        return B
    vb = lambda nm: (lambda c: B.vcol(VI[nm], c))

    def post_scale(v):
        def post(oc0, m, pc0, n, st_ap, rs):
            kb.op("dve", lambda e: e.tensor_scalar(st_ap, st_ap, v, None, ALU.mult), writes=[rs])
        return post
    with ExitStack() as es:
        sf = B.stage(es, "g_sf", F32)
        sb_ = B.stage(es, "g_sb", BF16)
        B.gemm(S["xmix0"], D, [(I["rwkv_wr"], D, B.epi_store(S["rT"], sf))], TT, tag="gr")
        B.gemm(S["xmix2"], D, [(I["rwkv_wk"], D, B.epi_store(S["kT"], sf))], TT, tag="gk")
        B.gemm(S["xmix3"], D, [(I["rwkv_wv"], D, B.epi_store(S["vT"], sf))], TT, tag="gv")
        B.gemm(S["xmix1"], D, [(I["rwkv_w1"][s_], LW, B.epi_store(S[f"lw{s_}"], sb_, func=AF.Tanh)) for s_ in range(2)], TT, tag="gw1")
        B.gemm(S["xmix4"], D, [(I["rwkv_a1"][s_], LW, B.epi_store(S[f"la{s_}"], sb_)) for s_ in range(2)], TT, tag="ga1")
        B.gemm(S["xmix5"], D, [(I["rwkv_g1"], LG, B.epi_store(S["lg"], sb_, func=AF.Sigmoid))], TT, tag="gg1")
        for s_, sl in enumerate("AB"):
            B.gemm(S[f"lw{s_}"], LW, [(I["rwkv_w2"][s_], D, B.epi_store(S[f"ld{s_}"], sf, func=AF.Sigmoid, bias=vb("w0" + sl),
                                                                        post=post_scale(NEG_E05)))], TT, tag=f"gw2{s_}")
            B.gemm(S[f"la{s_}"], LW, [(I["rwkv_a2"][s_], D, B.epi_store(S[f"al{s_}"], sf, func=AF.Sigmoid, bias=vb("a0" + sl)))],
                   TT, tag=f"ga2{s_}")
        B.gemm(S["lg"], LG, [(I["rwkv_g2"], D, B.epi_store(S["gT"], sf))], TT, tag="gg2")
    if upto == "proj":
        return B
    src0 = {"r": S["rT"], "k": S["kT"], "v": S["vT"], "ld": S["ld0"], "al": S["al0"], "alo": S["al1"]}
    src1 = {"r": S["rT"], "k": S["kT"], "v": S["vT"], "ld": S["ld1"], "al": S["al1"]}
    B.scan_pass(0, src0, VI, consts, S["yT0"], bonusT=S["bonusT"], state_out=S["stateA"])
    B.exchange_state(S["stateA"], S["gath"], S["state_in"], I["sel"])
    B.scan_pass(1, src1, VI, consts, S["yT1"], state_in=S["state_in"])
    B.rwkv_readout(S["yT0"], S["yT1"], S["bonusT"], S["gT"], S["zT"], consts)
    with ExitStack() as es:
        so = B.stage(es, "w_so", F32, nb=2)
        si = B.stage(es, "w_si", F32, nb=2)
        B.gemm(S["zT"], D, [(I["rwkv_wo"], D, B.epi_resid(xT, 0, 2, so, si))], TT, tag="gwo")
    if upto == "mixer0":
        return B
    B.ffn(xT, 0, I["ffn_wg"][0], I["ffn_wu"][0], I["ffn_wd"][0], S["h2"], S["hid"], TT)
    if "xT_l0" in B.debug:
        kb.dma("sp", B.dscr("xT_l0", [D, NTP]), xT)
        kb.barrier()
    if upto == "l0":
        return B
    D1 = {}
    for n, shp in (("gq8", [1, 512]), ("gk8", [1, 512]), ("lq1", [1, 64]), ("lk1", [1, 64]), ("lq2", [1, 64]), ("lk2", [1, 64]),
                   ("subg", [1, 128]), ("rope_cos", [NTP, 512]), ("rope_sin", [NTP, 512]),
                   ("diff_wqkv", [D, 3 * D]), ("diff_wo", [D, D])):
        D1[n] = B.din(n, shp)
    for n, shp in (("qT", [D, NOWN]), ("kT_ctx", [D, NCTX]), ("v_ctx", [NCTX, D])):
        D1[n] = B.dscr(n, shp, BF16)
    D1["kT_own"] = [B.dscr(f"kT_own{h}", [128, NOWN], BF16) for h in range(16)]
    D1["kT_all"] = [B.dscr(f"kT_all{h}", [256, NOWN], BF16) for h in range(16)]
    D1["v_own"] = [B.dscr(f"v_own{h}", [NOWN, 128], BF16) for h in range(16)]
    D1["v_all"] = [B.dscr(f"v_all{h}", [2 * NOWN, 128], BF16) for h in range(16)]
    hv1 = dram_fm(S["h2"])

    def consume1(ti, pc0, n, h, rh):
        kb.dma("sp", hv1[:, :, pc0:pc0 + n], h, reads=[rh])
    B.norm_tiles(xT, 1, 1, 0, TT, consume1, out_dt=BF16)
    B.qkv_phase(S["h2"], D1["diff_wqkv"], D1)
    rkv = B.exchange_kv(D1)
    B.attention(D1, rkv, S["zT"])
    with ExitStack() as es:
        so = B.stage(es, "w1_so", F32, nb=2)
        si = B.stage(es, "w1_si", F32, nb=2)
        B.gemm(S["zT"], D, [(D1["diff_wo"], D, B.epi_resid(xT, 1, 2, so, si))], TT[1:], tag="gwo1")
    B.ffn(xT, 1, I["ffn_wg"][1], I["ffn_wu"][1], I["ffn_wd"][1], S["h2"], S["hid"], TT[1:])
    B.fm_to_rows(xT, out)
    B.kb.barrier()
    return B


def _consts():
    idx = np.arange(128)
    mk = lambda f: np.tile(f(idx[None, :], idx[:, None]).astype(np.float32), (1, 2))
    masks = np.concatenate([mk(lambda f, p: f > p), mk(lambda f, p: f < p), mk(lambda f, p: f >= p), mk(lambda f, p: f <= p)], axis=1)
    bones = np.kron(np.eye(2), np.ones((64, 64))).astype(np.float32)
    ident2 = np.concatenate([np.eye(64), np.eye(64)], axis=0).astype(np.float32)
    id2h = np.concatenate([np.eye(128), np.eye(128)], axis=1).astype(np.float32)
    return np.ascontiguousarray(np.concatenate([masks, bones, ident2, id2h], axis=1))


def _lmask():
    idx = np.arange(128)
    out = np.zeros((128, 2, 7, 256), np.float32)
    for l in range(7):
        s = 2 ** l
        m = (((idx[:, None] // s) % 2 == 1) & ((idx[None, :] // s) == (idx[:, None] // s) - 1)).astype(np.float32)
        out[:, 0, l, :] = np.tile(m, (1, 2))
        out[:, 1, l, :] = np.tile(m.T, (1, 2))
    return np.ascontiguousarray(out.reshape(128, NLM))


def make_in_maps(inp):
    f = lambda a: np.ascontiguousarray(np.asarray(a, dtype=np.float32))
    x, c, ctx, c_ctx = f(inp["x"]), f(inp["c"]), f(inp["ctx"]), f(inp["c_ctx"])
    cst = _consts()
    ident = np.eye(128, dtype=np.float32)
    shared = {"ident": ident, "cst": cst, "lmask": _lmask(), "mod_w": f(inp["mod_w"]),
              "rwkv_wr": f(inp["rwkv_wr"][0]), "rwkv_wk": f(inp["rwkv_wk"][0]), "rwkv_wv": f(inp["rwkv_wv"][0]),
              "rwkv_wo": f(inp["rwkv_wo"][0]), "rwkv_g1": f(inp["rwkv_g1"][0]), "rwkv_g2": f(inp["rwkv_g2"][0]),
              "ffn_wg": f(inp["ffn_wg"]), "ffn_wu": f(inp["ffn_wu"]), "ffn_wd": f(inp["ffn_wd"])}
    dirw = {n: f(inp[n][0]) for n in ("rwkv_w1", "rwkv_w2", "rwkv_a1", "rwkv_a2", "rwkv_w0", "rwkv_a0")}
    dirw_sw = {n: np.ascontiguousarray(v[::-1]) for n, v in dirw.items()}
    maps = []
    for core in range(8):
        b, s = core // 2, core % 2
        own = x[b, s * NOWN:(s + 1) * NOWN]
        cx = ctx[b]
        if s == 1:
            own = own[::-1]
            cx = cx[::-1]
        xh = np.zeros((2, D), np.float32)
        xh[1] = x[b, NOWN] if s == 0 else x[b, NOWN - 1]
        cmask = np.ones((1, NTP), np.float32)
        cmask[0, [0, 257, 258]] = 0.0
        dw = dirw if s == 0 else dirw_sw
        vecs = {"c": c[b], "cctx": c_ctx}
        for l in range(2):
            for k in range(6):
                vecs[f"modb{l}_{k}"] = inp["mod_b"][l][k * D:(k + 1) * D]
            vecs[f"n1g{l}"] = inp["norm1_g"][l]
            vecs[f"n2g{l}"] = inp["norm2_g"][l]
        for k in range(6):
            vecs[f"mu{k}"] = inp["rwkv_mu"][0][k]
        vecs["w0A"], vecs["w0B"] = dw["rwkv_w0"][0], dw["rwkv_w0"][1]
        vecs["a0A"], vecs["a0B"] = dw["rwkv_a0"][0], dw["rwkv_a0"][1]
        vecs["kk"], vecs["ka"] = inp["rwkv_kk"][0], inp["rwkv_ka"][0]
        vecs["rk"] = np.asarray(inp["rwkv_rk"][0]).reshape(-1)
        vecs["lnw"], vecs["lnb"] = inp["rwkv_lnw"][0], inp["rwkv_lnb"][0]
        vec_all = np.ascontiguousarray(np.stack([f(vecs[n]) for n in VEC_NAMES]).reshape(-1, 128))
        m = dict(shared)
        m.update({"vec_all": vec_all, "cmask": cmask, "sel": np.array([[float(s == 1), float(s == 0)]], np.float32),
                  "xo": np.ascontiguousarray(own), "xc": np.ascontiguousarray(cx), "xh": xh,
                  "rwkv_w1": dw["rwkv_w1"], "rwkv_w2": dw["rwkv_w2"], "rwkv_a1": dw["rwkv_a1"], "rwkv_a2": dw["rwkv_a2"]})
        maps.append(m)
    return maps


TOK128 = [(1 + 128 * j, 128) for j in range(2)] + [(259 + 128 * j, 128) for j in range(16)]
NKT = 34


class Builder6(Builder5):
    def qkv_phase(self, h1_d, wqkv, D1):
        nc, kb = _B(self)
        from contextlib import ExitStack
        with ExitStack() as es:
            g8 = es.enter_context(self.sbt("q_g8", [128, 2, 512], F32))
            rg8 = Res()
            kb.dma("sp", g8[:, 0, :], D1["gq8"].partition_broadcast(128), writes=[rg8])
            kb.dma("sp", g8[:, 1, :], D1["gk8"].partition_broadcast(128), writes=[rg8])
            sq = es.enter_context(self.sbt("q_sq", [128, 512], F32))
            qn = es.enter_context(self.sbt("q_qn", [128, 2, 512], F32))
            rq = es.enter_context(self.sbt("q_rq", [128, 512], F32))
            ob = es.enter_context(self.sbt("q_ob", [128, 2, 512], BF16))
            cs_t = es.enter_context(self.sbt("q_cs", [128, 2, 2, 512], F32))
            ss = es.enter_context(self.sbt("q_ss", [128, 8], F32))
            tb = es.enter_context(self.sbt("q_tb", [64, 2, 8, 128], BF16))
            PT = es.enter_context(self.pst("q_pt", [128, 2, 512], F32))
            rsq, rss, rrq = Res(), Res(), Res()
            rqn, rob, rcs, rtb = [Res(), Res()], [Res(), Res()], [Res(), Res()], [Res(), Res()]
            rPT = [Res(), Res()]
            cnt = [0]

            def epi(col0, wc, pc0, n, ps, rps):
                isctx = pc0 < 258
                kind = col0 // 2048
                if kind == 0 and isctx:
                    return
                b = cnt[0] % 2
                cnt[0] += 1
                if kind == 2:
                    kb.op("act", lambda e: e.copy(ob[:, b, :], ps), reads=[rps], writes=[rob[b]])
                    cc = col0 - 4096
                    if isctx:
                        kb.dma("sp", D1["v_ctx"][pc0 - 1:pc0 - 1 + 128, cc:cc + 512], ob[:, b, :], reads=[rob[b]])
                    else:
                        for i4 in range(4):
                            kb.dma("sp", D1["v_own"][cc // 128 + i4][pc0 - 259:pc0 - 259 + 128, :], ob[:, b, i4 * 128:(i4 + 1) * 128],
                                   reads=[rob[b]])
                    return
                kb.op("act", lambda e: e.activation(out=sq[:], in_=ps, func=AF.Square), reads=[rps], writes=[rsq])
                kb.op("dve", lambda e: e.tensor_reduce(out=ss[:], in_=sq[:].rearrange("p (u d) -> p u d", d=64), axis=AX.X, op=ALU.add),
                      reads=[rsq], writes=[rss])
                kb.op("act", lambda e: e.activation(out=ss[:], in_=ss[:], func=AF.Sqrt, scale=1.0 / 64, bias=1e-6), writes=[rss])
                kb.op("dve", lambda e: e.reciprocal(ss[:], ss[:]), writes=[rss])
                if kind == 0:
                    kb.op("dve", lambda e: e.tensor_scalar(ss[:], ss[:], 0.125, None, ALU.mult), writes=[rss])
                kb.dma("sp", cs_t[:, b, 0, :], D1["rope_cos"][pc0:pc0 + 128, :], writes=[rcs[b]])
                kb.dma("sp", cs_t[:, b, 1, :], D1["rope_sin"][pc0:pc0 + 128, :], writes=[rcs[b]])
                for u in range(8):
                    us = slice(u * 64, (u + 1) * 64)
                    kb.op("dve", lambda e: e.scalar_tensor_tensor(out=qn[:, b, us], in0=ps[:, us], scalar=ss[:, u:u + 1],
                                                                  in1=g8[:, kind, us], op0=ALU.mult, op1=ALU.mult),
                          reads=[rps, rss, rg8], writes=[rqn[b]])
                v4 = lambda ap: ap.rearrange("p (a two f) -> p a two f", two=2, f=16)
                kb.op("dve", lambda e: e.tensor_scalar(v4(rq[:])[:, :, 0, :], v4(qn[:, b, :])[:, :, 1, :], -1.0, None, ALU.mult),
                      reads=[rqn[b]], writes=[rrq])
                kb.op("dve", lambda e: e.tensor_copy(v4(rq[:])[:, :, 1, :], v4(qn[:, b, :])[:, :, 0, :]),
                      reads=[rqn[b]], writes=[rrq])
                kb.op("dve", lambda e: e.tensor_tensor(qn[:, b, :], qn[:, b, :], cs_t[:, b, 0, :], ALU.mult),
                      reads=[rcs[b]], writes=[rqn[b]])
                kb.op("dve", lambda e: e.tensor_tensor(rq[:], rq[:], cs_t[:, b, 1, :], ALU.mult), reads=[rcs[b]], writes=[rrq])
                kb.op("dve", lambda e: e.tensor_tensor(ob[:, b, :], qn[:, b, :], rq[:], ALU.add), reads=[rrq, rqn[b]], writes=[rob[b]])
                for half in range(2):
                    for u4 in range(4):
                        u = half * 4 + u4
                        kb.op("pe", lambda e: e.matmul(PT[0:64, half, u4 * 128:(u4 + 1) * 128], ob[:, b, u * 64:(u + 1) * 64],
                                                       self.ident_b, start=True, stop=True),
                              reads=[rob[b], self.r_ident], writes=[rPT[half]])
                    if half == 0:
                        kb.op("act", lambda e: e.copy(tb[:, b, 0:4, :].rearrange("p u t -> p (u t)"), PT[0:64, 0, :]),
                              reads=[rPT[0]], writes=[rtb[b]])
                    else:
                        kb.op("dve", lambda e: e.tensor_copy(tb[:, b, 4:8, :].rearrange("p u t -> p (u t)"), PT[0:64, 1, :]),
                              reads=[rPT[1]], writes=[rtb[b]])
                u0 = (col0 % 2048) // 64
                if kind == 0:
                    dst = D1["qT"][u0 * 64:(u0 + 8) * 64, pc0 - 259:pc0 - 259 + 128]
                elif isctx:
                    dst = D1["kT_ctx"][u0 * 64:(u0 + 8) * 64, pc0 - 1:pc0 - 1 + 128]
                else:
                    for i4 in range(4):
                        dsth = D1["kT_own"][u0 // 2 + i4][:, pc0 - 259:pc0 - 259 + 128]
                        kb.dma("sp", dsth.rearrange("(u d) t -> d u t", d=64), tb[:, b, 2 * i4:2 * i4 + 2, :], reads=[rtb[b]])
                    return
                kb.dma("sp", dst.rearrange("(u d) t -> d u t", d=64), tb[:, b], reads=[rtb[b]])
            self.gemm(h1_d, D, [(wqkv, 3 * D, epi)], TT, x_stationary=True, tag="gqkv", npsum=4)

    def exchange_kv(self, D1):
        nc, kb = _B(self)
        kb.barrier()
        ccsem = kb._newsem("cckv")
        n = 0
        for h in range(16):
            for (a, b_) in ((D1["kT_own"][h], D1["kT_all"][h]), (D1["v_own"][h], D1["v_all"][h])):
                n += 1
                with nc.Block() as blk:
                    @blk.gpsimd
                    def _(g):
                        g.collective_compute("AllGather", ALU.bypass, replica_groups=[[0, 1], [2, 3], [4, 5], [6, 7]],
                                             ins=[a[:, :]], outs=[b_[:, :]]).then_inc(ccsem, 1)
                        g.wait_ge(ccsem, n)
        kb.waited["pool"][ccsem.name] = n
        r = Res()
        r.w = {ccsem.name: (ccsem, n, "cc")}
        return r

    def attention(self, D1, rkv, oT):
        nc, kb = _B(self)
        from contextlib import ExitStack
        with ExitStack() as es:
            sbt = lambda n, shp, dt: es.enter_context(self.sbt(n, shp, dt))
            KT = sbt("at_kt", [64, 2, NKT * 128], BF16)
            QT = sbt("at_qt", [64, 2, NOWN], BF16)
            Vs = sbt("at_v", [128, NKT, 129], BF16)
            PTs = sbt("at_p", [128, 2, 512], BF16)
            lv = sbt("at_lv", [128, 4, 64], F32)
            lam = sbt("at_lam", [128, 4], F32)
            sg = sbt("at_sg", [128, 128], F32)
            zz = sbt("at_z", [128, 4], F32)
            o0 = sbt("at_o0", [128, 128], F32)
            o1 = sbt("at_o1", [128, 128], F32)
            ob = sbt("at_ob", [128, 2, 128], BF16)
            ot = sbt("at_ot", [128, 2, 128], BF16)
            junk = sbt("at_junk", [128, 128], F32)
            PS = es.enter_context(self.pst("at_ps", [128, 2, 512], F32))
            PA = es.enter_context(self.pst("at_pa", [128, 4, 512], F32))
            PO = es.enter_context(self.pst("at_po", [128, 1, 512], F32))
            rKT, rQT, rV, rlam, rsg, rz, ro0, ro1, rPO, rj = (Res() for _ in range(10))
            rP, rPS, rob, rot = [Res(), Res()], [Res(), Res()], [Res(), Res()], [Res(), Res()]
            rPA = [Res() for _ in range(4)]
            for i, nmv in enumerate(("lq1", "lk1", "lq2", "lk2")):
                kb.dma("sp", lv[:, i, :], D1[nmv].partition_broadcast(128), writes=[rlam])
            kb.dma("sp", sg[:], D1["subg"].partition_broadcast(128), writes=[rsg])
            kb.op("dve", lambda e: e.tensor_scalar(sg[:], sg[:], 1.0 - LAMBDA_INIT1, None, ALU.mult), writes=[rsg])
            for i in range(2):
                kb.op("dve", lambda e: e.tensor_tensor(lv[:, 2 * i, :], lv[:, 2 * i, :], lv[:, 2 * i + 1, :], ALU.mult), writes=[rlam])
                kb.op("dve", lambda e: e.tensor_reduce(out=lam[:, i:i + 1], in_=lv[:, 2 * i, :], axis=AX.X, op=ALU.add), writes=[rlam])
            kb.op("act", lambda e: e.activation(out=lam[:, 0:2], in_=lam[:, 0:2], func=AF.Exp), writes=[rlam])
            kb.op("dve", lambda e: e.tensor_tensor(lam[:, 2:3], lam[:, 0:1], lam[:, 1:2], ALU.subtract), writes=[rlam])
            kb.op("dve", lambda e: e.tensor_scalar(lam[:, 3:4], lam[:, 2:3], LAMBDA_INIT1, -1.0, ALU.add, ALU.mult), writes=[rlam])
            kb.op("dve", lambda e: e.memset(Vs[:, :, 128:129], 1.0), writes=[rV])
            tcount = 0
            for h in range(16):
                for m in range(2):
                    rows = slice((2 * h + m) * 64, (2 * h + m + 1) * 64)
                    kb.dma("sp", KT[:, m, 0:256], D1["kT_ctx"][rows, :], writes=[rKT])
                    for r_ in range(2):
                        kb.dma("sp", KT[:, m, 256 + r_ * NOWN:256 + (r_ + 1) * NOWN], D1["kT_all"][h][r_ * 128 + m * 64:r_ * 128 + (m + 1) * 64, :],
                               reads=[rkv], writes=[rKT])
                    kb.dma("sp", QT[:, m, :], D1["qT"][rows, :], writes=[rQT])
                hc = slice(h * 128, (h + 1) * 128)
                kb.dma("sp", Vs[:, 0:2, 0:128], D1["v_ctx"][:, hc].rearrange("(kt p) e -> p kt e", p=128), writes=[rV])
                for r_ in range(2):
                    kb.dma("sp", Vs[:, 2 + 16 * r_:2 + 16 * (r_ + 1), 0:128],
                           D1["v_all"][h][r_ * NOWN:(r_ + 1) * NOWN, :].rearrange("(kt p) e -> p kt e", p=128), reads=[rkv], writes=[rV])
                for qb in range(4):
                    for m in range(2):
                        def smm(kt_, b_):
                            kb.op("pe", lambda e: e.matmul(PS[:, b_, :], KT[:, m, kt_ * 128:(kt_ + 1) * 128], QT[:, m, qb * 512:(qb + 1) * 512],
                                                           start=True, stop=True), reads=[rKT, rQT], writes=[rPS[b_]])
                        smm(0, tcount % 2)
                        for kt in range(NKT):
                            b = tcount % 2
                            tcount += 1
                            if kt + 1 < NKT:
                                smm(kt + 1, tcount % 2)
                            kb.op("act", lambda e: e.activation(out=PTs[:, b, :], in_=PS[:, b, :], func=AF.Exp),
                                  reads=[rPS[b]], writes=[rP[b]])
                            for qs in range(4):
                                bank = m * 2 + qs // 2
                                acc = PA[:, bank, (qs % 2) * 256:(qs % 2) * 256 + 129]
                                kb.op("pe", lambda e: e.matmul(acc, PTs[:, b, qs * 128:(qs + 1) * 128], Vs[:, kt, :],
                                                               start=(kt == 0), stop=(kt == NKT - 1)),
                                      reads=[rP[b], rV], writes=[rPA[bank]])
                    for qs in range(4):
                        a0 = PA[:, qs // 2, (qs % 2) * 256:(qs % 2) * 256 + 129]
                        a1 = PA[:, 2 + qs // 2, (qs % 2) * 256:(qs % 2) * 256 + 129]
                        ra0, ra1 = rPA[qs // 2], rPA[2 + qs // 2]
                        b = qs % 2
                        kb.op("dve", lambda e: e.tensor_copy(zz[:, 0:1], a0[:, 128:129]), reads=[ra0], writes=[rz])
                        kb.op("dve", lambda e: e.tensor_copy(zz[:, 1:2], a1[:, 128:129]), reads=[ra1], writes=[rz])
                        kb.op("dve", lambda e: e.reciprocal(zz[:, 0:2], zz[:, 0:2]), writes=[rz])
                        kb.op("dve", lambda e: e.tensor_tensor(zz[:, 1:2], zz[:, 1:2], lam[:, 3:4], ALU.mult), reads=[rlam], writes=[rz])
                        kb.op("dve", lambda e: e.tensor_scalar(o0[:], a0[:, 0:128], zz[:, 0:1], None, ALU.mult), reads=[ra0, rz], writes=[ro0])
                        kb.op("dve", lambda e: e.scalar_tensor_tensor(out=o1[:], in0=a1[:, 0:128], scalar=zz[:, 1:2], in1=o0[:],
                                                                      op0=ALU.mult, op1=ALU.add), reads=[ra1, rz, ro0], writes=[ro1])
                        kb.op("dve", lambda e: e.tensor_tensor(junk[:], o1[:], o1[:], ALU.mult), reads=[ro1], writes=[rj])
                        kb.op("dve", lambda e: e.tensor_reduce(out=zz[:, 2:3], in_=junk[:], axis=AX.X, op=ALU.add), reads=[rj], writes=[rz])
                        kb.op("act", lambda e: e.activation(out=zz[:, 2:3], in_=zz[:, 2:3], func=AF.Sqrt, scale=1.0 / 128, bias=1e-5), writes=[rz])
                        kb.op("dve", lambda e: e.reciprocal(zz[:, 2:3], zz[:, 2:3]), writes=[rz])
                        kb.op("dve", lambda e: e.scalar_tensor_tensor(out=ob[:, b, :], in0=o1[:], scalar=zz[:, 2:3], in1=sg[:],
                                                                      op0=ALU.mult, op1=ALU.mult), reads=[ro1, rz, rsg], writes=[rob[b]])
                        kb.op("pe", lambda e: e.matmul(PO[:, 0, 0:128], ob[:, b, :], self.ident_b, start=True, stop=True),
                              reads=[rob[b], self.r_ident], writes=[rPO])
                        kb.op("act", lambda e: e.copy(ot[:, b, :], PO[:, 0, 0:128]), reads=[rPO], writes=[rot[b]])
                        q0 = 259 + qb * 512 + qs * 128
                        kb.dma("sp", oT[hc, q0:q0 + 128], ot[:, b, :], reads=[rot[b]])
            kb.barrier()

    def fm_to_rows(self, xT, out):
        nc, kb = _B(self)
        xv = dram_fm(xT)
        with self.sbt("o_x", [128, 2, 16, 128], F32) as xt, self.sbt("o_r", [128, 2, 2048], F32) as rt, \
                self.pst("o_ps", [128, 4, 512], F32) as ps:
            rx, rr_ = [Res(), Res()], [Res(), Res()]
            rp = [Res() for _ in range(4)]
            for j in range(16):
                b = j % 2
                pc0 = 259 + j * 128
                kb.dma("sp", xt[:, b], xv[:, :, pc0:pc0 + 128], writes=[rx[b]])
                for q in range(4):
                    for i in range(4):
                        c = q * 4 + i
                        kb.op("pe", lambda e: e.matmul(ps[:, q, i * 128:(i + 1) * 128], xt[:, b, c, :], self.ident_f, start=True, stop=True),
                              reads=[rx[b], self.r_ident], writes=[rp[q]])
                    eng = "dve" if q % 2 == 0 else "act"
                    if eng == "dve":
                        kb.op("dve", lambda e: e.tensor_copy(rt[:, b, q * 512:(q + 1) * 512], ps[:, q, :]), reads=[rp[q]], writes=[rr_[b]])
                    else:
                        kb.op("act", lambda e: e.copy(rt[:, b, q * 512:(q + 1) * 512], ps[:, q, :]), reads=[rp[q]], writes=[rr_[b]])
                kb.dma("sp", out[j * 128:(j + 1) * 128, :], rt[:, b, :], reads=[rr_[b]])
            kb.barrier()


def _rope_tables(s):
    freqs = (10000.0 ** (-np.arange(16, dtype=np.float32) / 16)).astype(np.float32)
    cosE = np.ones((NTP, 64), np.float32)
    sinE = np.zeros((NTP, 64), np.float32)
    j = np.arange(NOWN)
    t = j if s == 0 else (2 * NOWN - 1 - j)
    row = (t // 64).astype(np.float32)[:, None] * freqs
    col = (t % 64).astype(np.float32)[:, None] * freqs
    cosE[259:259 + NOWN] = np.concatenate([np.cos(row), np.cos(row), np.cos(col), np.cos(col)], axis=1)
    sinE[259:259 + NOWN] = np.concatenate([np.sin(row), np.sin(row), np.sin(col), np.sin(col)], axis=1)
    return np.ascontiguousarray(np.tile(cosE, (1, 8))), np.ascontiguousarray(np.tile(sinE, (1, 8)))


def add_l1_maps(maps, inp):
    f = lambda a: np.ascontiguousarray(np.asarray(a, dtype=np.float32))
    ropes = [_rope_tables(0), _rope_tables(1)]
    extra = {"gq8": f(np.tile(np.asarray(inp["diff_qn"][0]), 8)[None, :]), "gk8": f(np.tile(np.asarray(inp["diff_kn"][0]), 8)[None, :]),
             "lq1": f(inp["diff_lq1"][0])[None, :], "lk1": f(inp["diff_lk1"][0])[None, :],
             "lq2": f(inp["diff_lq2"][0])[None, :], "lk2": f(inp["diff_lk2"][0])[None, :],
             "subg": f(inp["diff_subln"][0])[None, :], "diff_wqkv": f(inp["diff_wqkv"][0]), "diff_wo": f(inp["diff_wo"][0])}
    for core, m in enumerate(maps):
        m.update(extra)
        m["rope_cos"], m["rope_sin"] = ropes[core % 2]
    return maps


_PROG = None


def kernel(**inputs):
    global _PROG
    if _PROG is None:
        _PROG = build_program()
    B = _PROG
    maps = add_l1_maps(make_in_maps(inputs), inputs)
    used = set(B.inp.keys())
    maps = [{k: v for k, v in m.items() if k in used} for m in maps]
    res = run_bass_kernel_spmd(B.nc, maps, core_ids=list(range(8)))
    out = np.empty((4, 2 * NOWN, D), np.float32)
    for core in range(8):
        b, s = core // 2, core % 2
        o = np.asarray(res.results[core]["out"])
        out[b, s * NOWN:(s + 1) * NOWN] = o if s == 0 else o[::-1]
    return out
```

```python
import numpy as np
import ml_dtypes
import concourse.bass as bass
import concourse.mybir as mybir
from concourse.bass_utils import run_bass_kernel_spmd

F32 = mybir.dt.float32
BF16 = mybir.dt.bfloat16
AF = mybir.ActivationFunctionType
ALU = mybir.AluOpType
AX = mybir.AxisListType

D = 2048
DC = 16
NCTX = 256
NOWN = 2048
NT = NCTX + NOWN
NTP = NT + 4
DFF = 5632
FC = 44
LW = 96
LG = 256
H = 32
SEM_LIMIT = 30000
NWB = 3
import os as _os
NO_POOL = bool(_os.environ.get('NO_POOL', '1') == '1')


class Res:
    __slots__ = ("w", "r")

    def __init__(self):
        self.w = {}
        self.r = {}


class KB:
    def __init__(self, nc):
        self.nc = nc
        self.eng = {"pe": nc.tensor, "act": nc.scalar, "dve": nc.vector, "pool": nc.gpsimd, "sp": nc.sync}
        self.cur = {}
        self.nsem = 0
        self.waited = {e: {} for e in self.eng}
        self.slots = {"sp": [], "pool": []}
        self.slot_i = {"sp": 0, "pool": 0}
        for q, n in (("sp", 24), ("pool", 12)):
            for i in range(n):
                self.slots[q].append([self._newsem(f"d{q}{i}"), 0])
        for e in ("pe", "act", "dve", "pool"):
            self.cur[e] = [self._newsem(f"e{e}"), 0]
        self.n_ins = 0

    def _newsem(self, name):
        self.nsem += 1
        return self.nc.alloc_semaphore(f"{name}_{self.nsem}")

    def _wait(self, e, toks):
        w = self.waited[e]
        for key, (sem, val, src) in toks.items():
            if src == e and e == 'pe':
                continue
            if w.get(key, 0) < val:
                self.eng[e].wait_ge(sem, val)
                w[key] = val
                self.n_ins += 1

    @staticmethod
    def _merge(dst, src):
        for k, t in src.items():
            o = dst.get(k)
            if o is None or o[1] < t[1]:
                dst[k] = t

    def _deps(self, reads, writes):
        toks = {}
        for r in reads:
            self._merge(toks, r.w)
        for w in writes:
            self._merge(toks, w.w)
            self._merge(toks, w.r)
        return toks

    def _mark(self, tok, key, reads, writes):
        for r in reads:
            o = r.r.get(key)
            if o is None or o[1] < tok[1]:
                r.r[key] = tok
        for w in writes:
            w.w = {key: tok}
            w.r = {}

    def op(self, e, fn, reads=(), writes=()):
        if e == 'pool' and NO_POOL:
            e = 'dve'
        self._wait(e, self._deps(reads, writes))
        ins = fn(self.eng[e])
        c = self.cur[e]
        if c[1] >= SEM_LIMIT:
            c = self.cur[e] = [self._newsem(f"e{e}"), 0]
        c[1] += 1
        ins.then_inc(c[0], 1)
        self.n_ins += 1
        self._mark((c[0], c[1], e), c[0].name, reads, writes)
        return ins

    def dma(self, q, out, in_, reads=(), writes=(), **kw):
        toks = self._deps(reads, writes)
        sl = self.slots[q]
        i = self.slot_i[q]
        self.slot_i[q] = (i + 1) % len(sl)
        s = sl[i]
        if s[1] >= SEM_LIMIT:
            s[0] = self._newsem(f"d{q}x")
            s[1] = 0
        if s[1] > 0:
            toks[s[0].name] = (s[0], s[1], "dma")
        self._wait(q, toks)
        ins = self.eng[q].dma_start(out=out, in_=in_, **kw)
        s[1] += 16
        ins.then_inc(s[0], 16)
        self.n_ins += 1
        self._mark((s[0], s[1], "dma"), s[0].name, reads, writes)
        return ins

    def all_tokens(self):
        toks = {}
        for e, c in self.cur.items():
            if c[1] > 0:
                toks[c[0].name] = (c[0], c[1], "x")
        for q in self.slots:
            for s in self.slots[q]:
                if s[1] > 0:
                    toks[s[0].name] = (s[0], s[1], "dma")
        return toks

    def barrier(self, engines=("pe", "act", "dve", "pool", "sp")):
        toks = self.all_tokens()
        for e in engines:
            self._wait(e, toks)


def dram_fm(t):
    return t.rearrange("(c p) n -> p c n", p=128)


class Builder:
    def __init__(self, debug=None):
        self.debug = debug or []
        nc = self.nc = bass.Bass("TRN2", target_bir_lowering=False)
        self.kb = KB(nc)
        self.inp = {}
        self.out = {}

    def sbt(self, name, shape, dt):
        self._uid = getattr(self, "_uid", 0) + 1
        return self.nc.sbuf_tensor(f"{name}_u{self._uid}", shape, dt)

    def pst(self, name, shape, dt):
        self._uid = getattr(self, "_uid", 0) + 1
        return self.nc.psum_tensor(f"{name}_u{self._uid}", shape, dt)

    def din(self, name, shape, dt=F32):
        t = self.nc.dram_tensor(name, list(shape), dt, kind="ExternalInput").ap()
        self.inp[name] = t
        return t

    def dscr(self, name, shape, dt=F32):
        kind = "ExternalOutput" if name in self.debug else "Internal"
        t = self.nc.dram_tensor(name, list(shape), dt, kind=kind).ap()
        return t

    def load_consts(self, ident_f, ident_b, vec_all, nv):
        nc, kb = self.nc, self.kb
        self.ident_f = nc.alloc_sbuf_tensor("ident_f", [128, 128], F32).ap()
        self.ident_b = nc.alloc_sbuf_tensor("ident_b", [128, 128], BF16).ap()
        self.r_ident = Res()
        kb.dma("sp", self.ident_f, ident_f, writes=[self.r_ident])
        kb.dma("pool", self.ident_b, ident_f, writes=[self.r_ident])
        nrows = nv * 16
        self.vfm = nc.alloc_sbuf_tensor("vfm", [128, nrows], F32).ap()
        self.r_vfm = Res()
        with self.sbt("vrows", [128, 128], F32) as vrows, self.pst("vps", [128, 128], F32) as vps:
            r_rows, r_ps = Res(), Res()
            for g in range((nrows + 127) // 128):
                n = min(128, nrows - g * 128)
                kb.dma("sp", vrows[0:n, :], vec_all[g * 128:g * 128 + n, :], writes=[r_rows])
                kb.op("pe", lambda e: e.matmul(vps[:, 0:n], vrows[0:n, :], self.ident_f[0:n, 0:n], start=True, stop=True),
                      reads=[r_rows, self.r_ident], writes=[r_ps])
                kb.op("dve", lambda e: e.tensor_copy(self.vfm[:, g * 128:g * 128 + n], vps[:, 0:n]),
                      reads=[r_ps], writes=[self.r_vfm])
            kb.barrier()

    def vcol(self, idx, c):
        j = idx * 16 + c
        return self.vfm[:, j:j + 1]


TT = [(1, 256)] + [(259 + 512 * j, 512) for j in range(4)]


def _B(self):
    return self.nc, self.kb


class Builder2(Builder):
    def rows_to_fm(self, srcs, dst):
        nc, kb = _B(self)
        dv = dram_fm(dst)
        with self.sbt("t_x", [128, 2, 2048], F32) as xt, self.sbt("t_o", [128, 2, 16, 128], F32) as ot, \
                self.pst("t_ps", [128, 4, 4, 128], F32) as ps, self.sbt("t_z", [128, 16, 2], F32) as zt:
            rx = [Res(), Res()]
            ro = [Res(), Res()]
            rp = [Res() for _ in range(4)]
            rz = Res()
            kb.op("dve", lambda e: e.memset(zt[:], 0.0), writes=[rz])
            for pc in (0, 257):
                kb.dma("sp", dv[:, :, pc:pc + 1], zt[:, :, 0:1], reads=[rz], allow_slow_non_contiguous=True)
            it = 0
            for (src, n, pc0) in srcs:
                for r0 in range(0, n, 128):
                    m = min(128, n - r0)
                    b = it % 2
                    kb.dma("sp", xt[0:m, b, :], src[r0:r0 + m, :], writes=[rx[b]])
                    for q in range(4):
                        for j in range(4):
                            c = q * 4 + j
                            kb.op("pe", lambda e: e.matmul(ps[:, q, j, 0:m], xt[0:m, b, c * 128:(c + 1) * 128],
                                                           self.ident_f[0:m, 0:m], start=True, stop=True),
                                  reads=[rx[b], self.r_ident], writes=[rp[q]])
                        eng = "dve" if q % 2 == 0 else "act"
                        if eng == "dve":
                            kb.op("dve", lambda e: e.tensor_copy(ot[:, b, q * 4:(q + 1) * 4, 0:m], ps[:, q, :, 0:m]),
                                  reads=[rp[q]], writes=[ro[b]])
                        else:
                            kb.op("act", lambda e: e.copy(ot[:, b, q * 4:(q + 1) * 4, 0:m], ps[:, q, :, 0:m]),
                                  reads=[rp[q]], writes=[ro[b]])
                    kb.dma("sp", dv[:, :, pc0 + r0:pc0 + r0 + m], ot[:, b, :, 0:m], reads=[ro[b]], allow_slow_non_contiguous=(m < 8))
                    it += 1
            kb.barrier()

    def adaln(self, mod_w, vi):
        nc, kb = _B(self)
        self.modfm = nc.alloc_sbuf_tensor("modfm", [128, 2 * 6 * 16 * 2], F32).ap().rearrange(
            "p (l k c j) -> p l k c j", l=2, k=6, c=16)
        self.r_mod = Res()
        with self.sbt("a_sc", [128, 16, 2], BF16) as sc, self.sbt("a_w", [128, 2, 16, 512], BF16) as wt, \
                self.pst("a_ps", [128, 2, 512], F32) as ps:
            r_sc = Res()
            rw = [Res(), Res()]
            rp = [Res(), Res()]
            for j, nm in enumerate(("c", "cctx")):
                i0 = vi[nm] * 16
                kb.op("act", lambda e: e.activation(out=sc[:, :, j], in_=self.vfm[:, i0:i0 + 16], func=AF.Silu),
                      reads=[self.r_vfm], writes=[r_sc])
            it = 0
            for l in range(2):
                wv = mod_w[l].rearrange("(kc p) n -> p kc n", p=128)
                for g in range(24):
                    b = it % 2
                    kb.dma("pool", wt[:, b], wv[:, :, g * 512:(g + 1) * 512], writes=[rw[b]])
                    for j in range(4):
                        for kc in range(16):
                            kb.op("pe", lambda e: e.matmul(ps[:, b, 2 * j:2 * j + 2], wt[:, b, kc, j * 128:(j + 1) * 128], sc[:, kc, :],
                                                           start=(kc == 0), stop=(kc == 15)),
                                  reads=[rw[b], r_sc], writes=[rp[b]])
                    for j in range(4):
                        oc = g * 4 + j
                        k, cc = oc // 16, oc % 16
                        kb.op("dve", lambda e: e.tensor_scalar(self.modfm[:, l, k, cc, :], ps[:, b, 2 * j:2 * j + 2],
                                                               self.vcol(vi[f"modb{l}_{k}"], cc), None, ALU.add),
                              reads=[rp[b], self.r_vfm], writes=[self.r_mod])
                    it += 1
            for l in range(2):
                for (k, nm) in ((1, f"n1g{l}"), (4, f"n2g{l}")):
                    i0 = vi[nm] * 16
                    for j in range(2):
                        kb.op("dve", lambda e: e.scalar_tensor_tensor(
                            out=self.modfm[:, l, k, :, j], in0=self.modfm[:, l, k, :, j], scalar=1.0,
                            in1=self.vfm[:, i0:i0 + 16], op0=ALU.add, op1=ALU.mult),
                            reads=[self.r_vfm], writes=[self.r_mod])
            kb.barrier()

    def mcol(self, l, k, c, j):
        return self.modfm[:, l, k, c, j:j + 1]

    def norm_tiles(self, xsrc, l, kg, ksh, tiles, consume, halo=0, cmask=None, out_dt=F32, maxn=512):
        nc, kb = _B(self)
        xv = dram_fm(xsrc)
        W = maxn + 2 * halo
        with self.sbt("n_x", [128, 2, 16, W], F32) as xt, self.sbt("n_sq", [128, 16, W], BF16) as sq, \
                self.sbt("n_h", [128, 2, 16, W], out_dt) as ht, self.sbt("n_r", [128, 2, W], F32) as rs, \
                self.sbt("n_ones", [128, 128], BF16) as ones, self.pst("n_ps", [128, 2, 512], F32) as ps, \
                self.pst("n_ps2", [128, 2, 512], F32) as ps2:
            rx, rh, rr, rp = [Res(), Res()], [Res(), Res()], [Res(), Res()], [Res(), Res()]
            rsq, rones = Res(), Res()
            kb.op("dve", lambda e: e.memset(ones[:], 1.0), writes=[rones])
            for ti, (pc0, n) in enumerate(tiles):
                b = ti % 2
                jj = 1 if pc0 < 258 else 0
                w = n + 2 * halo
                kb.dma("sp", xt[:, b, :, 0:w], xv[:, :, pc0 - halo:pc0 - halo + w], writes=[rx[b]])
                kb.op("act", lambda e: e.activation(out=sq[:, :, 0:w], in_=xt[:, b, :, 0:w], func=AF.Square),
                      reads=[rx[b]], writes=[rsq])
                parts = [(ps[:, b, 0:n], halo, n)]
                if halo:
                    parts += [(ps2[:, b, 0:1], 0, 1), (ps2[:, b, 1:2], w - 1, 1)]
                for (pp, c0, cn) in parts:
                    for c in range(16):
                        kb.op("pe", lambda e: e.matmul(pp, ones[:], sq[:, c, c0:c0 + cn], start=(c == 0), stop=(c == 15)),
                              reads=[rsq, rones], writes=[rp[b]])
                for (pp, c0, cn) in parts:
                    kb.op("act", lambda e: e.activation(out=rs[:, b, c0:c0 + cn], in_=pp, func=AF.Sqrt,
                                                        scale=1.0 / D, bias=1e-6),
                          reads=[rp[b]], writes=[rr[b]])
                kb.op("dve", lambda e: e.reciprocal(rs[:, b, 0:w], rs[:, b, 0:w]), writes=[rr[b]])
                if cmask is not None:
                    kb.op("dve", lambda e: e.tensor_tensor(rs[:, b, 0:w], rs[:, b, 0:w],
                                                           cmask[0][:, pc0 - halo:pc0 - halo + w], ALU.mult),
                          reads=[cmask[1]], writes=[rr[b]])
                for c in range(16):
                    eng = "dve"
                    kb.op(eng, lambda e: e.scalar_tensor_tensor(out=xt[:, b, c, 0:w], in0=xt[:, b, c, 0:w],
                                                                scalar=self.mcol(l, kg, c, jj), in1=rs[:, b, 0:w],
                                                                op0=ALU.mult, op1=ALU.mult),
                          reads=[rr[b], self.r_mod], writes=[rx[b]])
                    if cmask is None:
                        kb.op("act", lambda e: e.activation(out=ht[:, b, c, 0:w], in_=xt[:, b, c, 0:w], func=AF.Identity,
                                                            bias=self.mcol(l, ksh, c, jj), scale=1.0),
                              reads=[rx[b], self.r_mod], writes=[rh[b]])
                    else:
                        kb.op(eng, lambda e: e.scalar_tensor_tensor(out=ht[:, b, c, 0:w],
                                                                    in0=cmask[0][:, pc0 - halo:pc0 - halo + w],
                                                                    scalar=self.mcol(l, ksh, c, jj), in1=xt[:, b, c, 0:w],
                                                                    op0=ALU.mult, op1=ALU.add),
                              reads=[rx[b], self.r_mod, cmask[1]], writes=[rh[b]])
                consume(ti, pc0, n, ht[:, b, :, 0:w], rh[b])
            kb.barrier()


TT256 = [(1, 256)] + [(259 + 256 * j, 256) for j in range(8)]


class Builder3(Builder2):
    def make_mixer(self, xmix, vi, pool):
        nc, kb = _B(self)
        tmp, xo = pool["tmp"], pool["xo"]
        rt, ro = Res(), [Res(), Res()]
        cnt = [0]

        def consume(ti, pc0, n, h, rh):
            hm = h[:, :, 1:n + 1]
            kb.op("pool", lambda e: e.tensor_tensor(tmp[:, :, 0:n], h[:, :, 0:n], h[:, :, 2:n + 2], ALU.add),
                  reads=[rh], writes=[rt])
            kb.op("dve", lambda e: e.scalar_tensor_tensor(out=tmp[:, :, 0:n], in0=tmp[:, :, 0:n], scalar=0.5, in1=hm,
                                                          op0=ALU.mult, op1=ALU.subtract), reads=[rh], writes=[rt])
            for k in range(6):
                b = cnt[0] % 2
                cnt[0] += 1
                for c in range(16):
                    kb.op("dve", lambda e: e.scalar_tensor_tensor(out=xo[:, b, c, 0:n], in0=tmp[:, c, 0:n],
                                                                  scalar=self.vcol(vi[f"mu{k}"], c), in1=hm[:, c, :],
                                                                  op0=ALU.mult, op1=ALU.add),
                          reads=[rt, rh, self.r_vfm], writes=[ro[b]])
                kb.dma("sp", dram_fm(xmix[k])[:, :, pc0:pc0 + n], xo[:, b, :, 0:n], reads=[ro[b]])
        return consume

    def gemm(self, xsrc, K, jobs, tiles, x_stationary=False, wcols=512, tag="g", npsum=6):
        nc, kb = _B(self)
        kp = min(K, 128)
        KC = (K + 127) // 128
        lo = min(t[0] for t in tiles)
        hi = max(t[0] + t[1] for t in tiles)
        span = hi - lo
        with self.sbt(tag + "_x", [kp, KC, span], BF16) as XT, self.sbt(tag + "_w", [kp, NWB, KC, wcols], BF16) as WT, \
                self.pst(tag + "_ps", [128, npsum, 512], F32) as PS:
            rx = Res()
            rw = [Res() for _ in range(NWB)]
            rp = [Res() for _ in range(npsum)]
            xv = xsrc.rearrange("(kc p) n -> p kc n", p=kp)
            step = max(1, KC // 4)
            for k0 in range(0, KC, step):
                kb.dma("sp", XT[:, k0:k0 + step, :], xv[:, k0:k0 + step, lo:hi], writes=[rx])
            wi = 0
            pi = 0
            for (W, E, epi) in jobs:
                wv = W.rearrange("(kc p) n -> p kc n", p=kp)
                for g0 in range(0, E, wcols):
                    wc = min(wcols, E - g0)
                    b = wi % NWB
                    wi += 1
                    kb.dma("pool", WT[:, b, :, 0:wc], wv[:, :, g0:g0 + wc], writes=[rw[b]])
                    for (pc0, n) in tiles:
                        if not x_stationary:
                            for j0 in range(0, wc, 128):
                                m = min(128, wc - j0)
                                q = pi % npsum
                                pi += 1
                                for kc in range(KC):
                                    kb.op("pe", lambda e: e.matmul(PS[0:m, q, 0:n], WT[:, b, kc, j0:j0 + m],
                                                                   XT[:, kc, pc0 - lo:pc0 - lo + n],
                                                                   start=(kc == 0), stop=(kc == KC - 1)),
                                          reads=[rw[b], rx], writes=[rp[q]])
                                epi(g0 + j0, m, pc0, n, PS[0:m, q, 0:n], rp[q])
                        else:
                            for t0 in range(0, n, 128):
                                q = pi % npsum
                                pi += 1
                                for kc in range(KC):
                                    kb.op("pe", lambda e: e.matmul(PS[:, q, 0:wc], XT[:, kc, pc0 - lo + t0:pc0 - lo + t0 + 128],
                                                                   WT[:, b, kc, 0:wc], start=(kc == 0), stop=(kc == KC - 1)),
                                          reads=[rw[b], rx], writes=[rp[q]])
                                epi(g0, wc, pc0 + t0, 128, PS[:, q, 0:wc], rp[q])
            kb.barrier()

    def epi_store(self, dst, stage, func=AF.Identity, bias=None, scale=1.0, post=None):
        nc, kb = _B(self)
        st, rs = stage
        cnt = [0]

        def epi(oc0, m, pc0, n, ps, rps):
            b = cnt[0] % len(rs)
            cnt[0] += 1
            bb = bias(oc0 // 128)[0:m, :] if bias is not None else 0.0
            rd = [rps] + ([self.r_vfm, self.r_mod] if bias is not None else [])
            kb.op("act", lambda e: e.activation(out=st[0:m, b, 0:n], in_=ps, func=func, bias=bb, scale=scale),
                  reads=rd, writes=[rs[b]])
            if post is not None:
                post(oc0, m, pc0, n, st[0:m, b, 0:n], rs[b])
            kb.dma("sp", dst[oc0:oc0 + m, pc0:pc0 + n], st[0:m, b, 0:n], reads=[rs[b]])
        return epi


CH_COLS = [1, 129] + [259 + 128 * j for j in range(16)]


class Builder4(Builder3):
    def scan_pass(self, slot, src, vi, consts, yT, bonusT=None, state_out=None, state_in=None):
        nc, kb = _B(self)
        masks, rmask, bones, ident2 = consts["masks"], consts["rmask"], consts["bones"], consts["ident2"]
        rc = consts["rc"]
        m_st = masks[:, 0] if slot == 0 else masks[:, 1]
        m_ts = masks[:, 1] if slot == 0 else masks[:, 0]
        m_si = masks[:, 2] if slot == 0 else masks[:, 3]
        fr = ["r", "k", "v", "ld", "al"] + (["alo"] if slot == 0 else [])
        sfx = f"s{slot}"
        from contextlib import ExitStack
        with ExitStack() as es:
            def sb(name, shape, dt):
                return es.enter_context(self.sbt(sfx + name, shape, dt))
            def pst(name, shape):
                return es.enter_context(self.pst(sfx + name, shape, F32))
            row = {n: sb("r_" + n, [128, NTP], F32) for n in fr if n != "alo"}
            rr = {n: Res() for n in fr if n != "alo"}
            kkn, W1, W2, W3 = (sb(n, [128, NTP], F32) for n in ("kkn", "W1", "W2", "W3"))
            ks = sb("ks", [128, NTP], F32)
            yrow = sb("T1", [128, NTP], F32)
            At, Bt, Kt, Rt, Bb, Kb_, Vb = (sb(n, [128, NTP], BF16) for n in ("At", "Bt", "Kt", "Rt", "Bb", "Kb", "Vb"))
            ones = sb("ones", [128, 128], F32)
            lmask = consts["lmask"]
            lm_ts = lmask[:, 0] if slot == 0 else lmask[:, 1]
            lm_st = lmask[:, 1] if slot == 0 else lmask[:, 0]
            id2h = consts["id2h"]
            Hf = sb("Hf", [64, 2, 64], F32)
            Hb = sb("Hb", [64, 2, 64], BF16)
            ych = sb("ych", [64, 2, 2, 128], F32)
            rych = [Res(), Res()]
            Ath = [sb(f"Ath{i}", [128, NTP], BF16) for i in range(2)]
            Bth = [sb(f"Bth{i}", [128, NTP], BF16) for i in range(2)]
            Kth = [sb(f"Kth{i}", [128, NTP], BF16) for i in range(2)]
            rAth, rBth, rKth = [Res(), Res()], [Res(), Res()], [Res(), Res()]
            PG = pst("PG", [128, 2, 512])
            PD = pst("PD", [128, 3, 512])
            PH = pst("PH", [128, 2, 512])
            PHh = pst("PHh", [128, 1, 512])
            rPHh = Res()
            rrow = {n: Res() for n in ("kkn", "W1", "W2", "W3", "ks", "y", "At", "Bt", "Kt", "Rt", "Bb", "Kb", "Vb")}
            if slot == 0:
                row["alo"] = yrow
                rr["alo"] = rrow["y"]
            r1 = Res()
            rHf, rHb = Res(), Res()

            def chunk_set(i):
                t = {}
                for n, shp, dt_ in (("AkT", [128, 2, 128], BF16), ("ArbT", [128, 2, 128], BF16), ("ArkT", [128, 2, 128], BF16),
                                    ("TM", [128, 4, 128], BF16), ("Xf", [128, 2, 128], BF16), ("Xb", [128, 2, 128], BF16),
                                    ("No", [128, 7, 256], BF16), ("NoT", [128, 7, 256], BF16), ("Tm", [128, 2, 256], BF16),
                                    ("TTm", [128, 2, 256], BF16), ("A1", [128, 256], BF16), ("B1", [128, 256], BF16),
                                    ("MTs", [64, 2, 64], BF16), ("pC2", [64, 2], F32), ("Rh", [64, 2, 128], BF16)):
                    t[n] = sb(f"{n}{i}", shp, dt_)
                for n in ("rAk", "rArb", "rArk", "rTM", "rXf", "rXb", "rMT", "rRh", "rNo", "rNoT", "rA1", "rB1"):
                    t[n] = Res()
                t["rT"] = [Res(), Res()]
                t["rTT"] = [Res(), Res()]
                return t
            csets = [chunk_set(0), chunk_set(1)]
            rPG = [Res() for _ in range(2)]
            rPD = [Res() for _ in range(3)]
            rPH = [Res(), Res()]
            gi = [0]
            di = [0]

            def pg():
                i = gi[0] % 2
                gi[0] += 1
                return PG[:, i, 0:256], rPG[i]

            def pd():
                i = di[0] % 3
                di[0] += 1
                return PD[:, i, 0:256], rPD[i]

            kb.op("dve", lambda e: e.memset(ones[:], 1.0), writes=[r1])
            kb.op("dve", lambda e: e.memset(W3[:], 0.0), writes=[rrow["W3"]])
            V = lambda nm, c: self.vcol(vi[nm], c)
            for c in range(getattr(self, 'scan_nc', 16)):
                for n in fr:
                    kb.dma("sp", row[n][:], src[n][c * 128:(c + 1) * 128, :], writes=[rr[n]])
                r_, k_, v_, ld, al = (row[n] for n in ("r", "k", "v", "ld", "al"))
                kb.op("dve", lambda e: e.tensor_scalar(kkn[:], k_[:], V("kk", c), None, ALU.mult),
                      reads=[rr["k"], self.r_vfm], writes=[rrow["kkn"]])
                kb.op("act", lambda e: e.activation(out=W1[:], in_=kkn[:], func=AF.Square),
                      reads=[rrow["kkn"]], writes=[rrow["W1"]])
                for p0 in range(0, NTP, 256):
                    n = min(256, NTP - p0)
                    ps, rps = pg()
                    kb.op("pe", lambda e: e.matmul(ps[:, 0:n], bones, W1[:, p0:p0 + n], start=True, stop=True),
                          reads=[rrow["W1"], rc], writes=[rps])
                    kb.op("dve", lambda e: e.tensor_scalar(W2[:, p0:p0 + n], ps[:, 0:n], 1e-24, None, ALU.max),
                          reads=[rps], writes=[rrow["W2"]])
                kb.op("act", lambda e: e.activation(out=W2[:], in_=W2[:], func=AF.Sqrt), writes=[rrow["W2"]])
                kb.op("dve", lambda e: e.reciprocal(W2[:], W2[:]), writes=[rrow["W2"]])
                kb.op("dve", lambda e: e.tensor_tensor(kkn[:], kkn[:], W2[:], ALU.mult),
                      reads=[rrow["W2"]], writes=[rrow["kkn"]])
                kb.op("dve", lambda e: e.tensor_scalar(ks[:], al[:], -1.0, V("ka", c), ALU.add, ALU.mult),
                      reads=[rr["al"], self.r_vfm], writes=[rrow["ks"]])
                kb.op("dve", lambda e: e.scalar_tensor_tensor(out=ks[:], in0=ks[:], scalar=1.0, in1=k_[:],
                                                              op0=ALU.add, op1=ALU.mult),
                      reads=[rr["k"]], writes=[rrow["ks"]])
                if slot == 0:
                    alo = row["alo"]
                    kb.op("dve", lambda e: e.tensor_scalar(W1[:], alo[:], -1.0, V("ka", c), ALU.add, ALU.mult),
                          reads=[rr["alo"], self.r_vfm], writes=[rrow["W1"]])
                    kb.op("dve", lambda e: e.scalar_tensor_tensor(out=W1[:], in0=W1[:], scalar=1.0, in1=k_[:],
                                                                  op0=ALU.add, op1=ALU.mult),
                          reads=[rr["k"]], writes=[rrow["W1"]])
                    kb.op("pool", lambda e: e.tensor_tensor(W1[:], W1[:], ks[:], ALU.add),
                          reads=[rrow["ks"]], writes=[rrow["W1"]])
                    kb.op("dve", lambda e: e.scalar_tensor_tensor(out=W1[:], in0=r_[:], scalar=V("rk", c), in1=W1[:],
                                                                  op0=ALU.mult, op1=ALU.mult),
                          reads=[rr["r"], self.r_vfm], writes=[rrow["W1"]])
                    for p0 in range(0, NTP, 256):
                        n = min(256, NTP - p0)
                        ps, rps = pg()
                        kb.op("pe", lambda e: e.matmul(ps[:, 0:n], bones, W1[:, p0:p0 + n], start=True, stop=True),
                              reads=[rrow["W1"], rc], writes=[rps])
                        kb.op("dve", lambda e: e.tensor_tensor(W3[:, p0:p0 + n], ps[:, 0:n], v_[:, p0:p0 + n], ALU.mult),
                              reads=[rps, rr["v"]], writes=[rrow["W3"]])
                    kb.dma("sp", bonusT[c * 128:(c + 1) * 128, :], W3[:], reads=[rrow["W3"]])
                import os
                for cj in CH_COLS:
                    if os.environ.get('NO_SCAN'):
                        kb.op("dve", lambda e: e.tensor_copy(W1[:, cj:cj + 128], ld[:, cj:cj + 128]),
                              reads=[rr["ld"], r1], writes=[rrow["W1"]])
                        continue
                    kb.op("dve", lambda e: e.tensor_tensor_scan(W1[:, cj:cj + 128], ones[:], ld[:, cj:cj + 128], 0.0,
                                                                ALU.mult, ALU.add),
                          reads=[rr["ld"], r1], writes=[rrow["W1"]])
                if slot == 0:
                    L, rL = W1, rrow["W1"]
                else:
                    kb.op("pool", lambda e: e.tensor_tensor(W2[:], ld[:], W1[:], ALU.subtract),
                          reads=[rr["ld"], rrow["W1"]], writes=[rrow["W2"]])
                    for cj in CH_COLS:
                        kb.op("dve", lambda e: e.tensor_scalar(W3[:, cj:cj + 128], W2[:, cj:cj + 128],
                                                               W1[:, cj + 127:cj + 128], None, ALU.add),
                              reads=[rrow["W2"], rrow["W1"]], writes=[rrow["W3"]])
                    L, rL = W3, rrow["W3"]
                kb.op("pool", lambda e: e.tensor_tensor(W2[:], L[:], ld[:], ALU.subtract),
                      reads=[rL, rr["ld"]], writes=[rrow["W2"]])
                kb.op("act", lambda e: e.activation(out=W2[:], in_=W2[:], func=AF.Exp), writes=[rrow["W2"]])
                kb.op("act", lambda e: e.activation(out=yrow[:], in_=L[:], func=AF.Exp, scale=-1.0),
                      reads=[rL], writes=[rrow["y"]])
                kb.op("act", lambda e: e.activation(out=L[:], in_=L[:], func=AF.Exp), writes=[rL])
                kb.op("dve", lambda e: e.scalar_tensor_tensor(out=At[:], in0=kkn[:], scalar=-1.0, in1=W2[:],
                                                              op0=ALU.mult, op1=ALU.mult),
                      reads=[rrow["kkn"], rrow["W2"]], writes=[rrow["At"]])
                kb.op("pool", lambda e: e.tensor_tensor(kkn[:], kkn[:], al[:], ALU.mult),
                      reads=[rr["al"], rrow["At"]], writes=[rrow["kkn"]])
                kb.op("dve", lambda e: e.tensor_tensor(Bt[:], kkn[:], yrow[:], ALU.mult),
                      reads=[rrow["kkn"], rrow["y"]], writes=[rrow["Bt"]])
                kb.op("pool", lambda e: e.tensor_tensor(Kt[:], ks[:], yrow[:], ALU.mult),
                      reads=[rrow["ks"], rrow["y"]], writes=[rrow["Kt"]])
                kb.op("dve", lambda e: e.tensor_tensor(Rt[:], r_[:], L[:], ALU.mult),
                      reads=[rr["r"], rL], writes=[rrow["Rt"]])
                kb.op("act", lambda e: e.copy(Vb[:], v_[:]), reads=[rr["v"]], writes=[rrow["Vb"]])
                for cj in CH_COLS:
                    pcx = cj + 127 if slot == 0 else cj
                    kb.op("dve", lambda e: e.tensor_scalar(Bb[:, cj:cj + 128], Bt[:, cj:cj + 128], L[:, pcx:pcx + 1], None, ALU.mult),
                          reads=[rrow["Bt"], rL], writes=[rrow["Bb"]])
                    kb.op("pool", lambda e: e.tensor_scalar(Kb_[:, cj:cj + 128], Kt[:, cj:cj + 128], L[:, pcx:pcx + 1], None, ALU.mult),
                          reads=[rrow["Kt"], rL], writes=[rrow["Kb"]])
                for hh in range(2):
                    hm = bones[:, hh * 64:hh * 64 + 1]
                    for (srcr, rs_, dstl, rd_) in ((At, rrow["At"], Ath, rAth), (Bt, rrow["Bt"], Bth, rBth), (Kt, rrow["Kt"], Kth, rKth)):
                        kb.op("dve", lambda e: e.tensor_scalar(dstl[hh][:], srcr[:], hm, None, ALU.mult),
                              reads=[rs_, rc], writes=[rd_[hh]])
                import os
                stop = int(os.environ.get('SCAN_STOP', '99'))
                order = list(range(18)) if slot == 0 else [1, 0] + list(range(17, 1, -1))
                kb.op("dve", lambda e: e.memset(Hf[:], 0.0), writes=[rHf])
                kb.op("dve", lambda e: e.memset(Hb[:], 0.0), writes=[rHb])
                HS = [slice(0, 64), slice(64, 128)]
                def chunk_pre(oi, j):
                    cj = CH_COLS[j]
                    cs = slice(cj, cj + 128)
                    T_ = csets[oi % 2]
                    AkT, ArbT, ArkT, TM, Xf, Xb, No, NoT, Tm, TTm, A1, B1, MTs, pC2, Rh = (T_[n] for n in (
                        "AkT", "ArbT", "ArkT", "TM", "Xf", "Xb", "No", "NoT", "Tm", "TTm", "A1", "B1", "MTs", "pC2", "Rh"))
                    rAk, rArb, rArk, rTM, rXf, rXb, rMT, rRh, rNo, rNoT, rA1, rB1, rT, rTT = (T_[n] for n in (
                        "rAk", "rArb", "rArk", "rTM", "rXf", "rXb", "rMT", "rRh", "rNo", "rNoT", "rA1", "rB1", "rT", "rTT"))

                    def gram(lh, rlh, rh, rrh, mask, dst, rdst):
                        ps, rps = pg()
                        for hh in range(2):
                            kb.op("pe", lambda e: e.matmul(ps[:, hh * 128:(hh + 1) * 128], lh[hh][:, cs], rh[:, cs],
                                                           start=True, stop=True),
                                  reads=[rlh[hh], rrh], writes=[rps])
                        kb.op("dve", lambda e: e.tensor_tensor(dst, ps[:], mask, ALU.mult), reads=[rps, rc], writes=[rdst])
                    for (lh, rlh, rh, rrh, lm, dst, rdst) in ((Bth, rBth, At, rrow["At"], lm_st, NoT, rNoT), (Ath, rAth, Bt, rrow["Bt"], lm_ts, No, rNo)):
                        ps, rps = pg()
                        for hh in range(2):
                            kb.op("pe", lambda e: e.matmul(ps[:, hh * 128:(hh + 1) * 128], lh[hh][:, cs], rh[:, cs], start=True, stop=True),
                                  reads=[rlh[hh], rrh], writes=[rps])
                        for l in range(7):
                            kb.op("dve", lambda e: e.tensor_tensor(dst[:, l, :], ps[:], lm[:, l, :], ALU.mult), reads=[rps, rc], writes=[rdst])
                    gram(Kth, rKth, At, rrow["At"], m_st, AkT[:].rearrange("p h t -> p (h t)"), rAk)
                    gram(Bth, rBth, Rt, rrow["Rt"], m_si, ArbT[:].rearrange("p h t -> p (h t)"), rArb)
                    gram(Kth, rKth, Rt, rrow["Rt"], m_si, ArkT[:].rearrange("p h t -> p (h t)"), rArk)
                    yield
                    for q2, (srcr, rsrc) in enumerate(((At, rrow["At"]), (Bb, rrow["Bb"]), (Kb_, rrow["Kb"]), (Vb, rrow["Vb"]))):
                        if q2 % 2 == 0:
                            ps, rps = pg()
                        kb.op("pe", lambda e: e.matmul(ps[:, (q2 % 2) * 128:(q2 % 2 + 1) * 128], srcr[:, cs], self.ident_b,
                                                       start=True, stop=True),
                              reads=[rsrc, self.r_ident], writes=[rps])
                        if q2 % 2 == 1:
                            kb.op("act", lambda e: e.copy(TM[:, q2 - 1:q2 + 1, :].rearrange("p a t -> p (a t)"), ps[:]),
                                  reads=[rps], writes=[rTM])
                    ps, rps = pg()
                    for hh in range(2):
                        kb.op("pe", lambda e: e.matmul(ps[:, hh * 64:(hh + 1) * 64], AkT[:, hh, :], TM[:, 3, HS[hh]],
                                                       start=True, stop=True),
                              reads=[rAk, rTM], writes=[rps])
                    for hh in range(2):
                        kb.op("dve", lambda e: e.tensor_copy(Xb[:, hh, 0:64], TM[:, 0, HS[hh]]), reads=[rTM], writes=[rXb])
                        kb.op("dve", lambda e: e.tensor_copy(Xb[:, hh, 64:128], ps[:, hh * 64:(hh + 1) * 64]),
                              reads=[rps], writes=[rXb])
                    yield
                    kb.op("dve", lambda e: e.tensor_tensor(Tm[:, 0, :], No[:, 0, :], id2h, ALU.add), reads=[rNo, rc], writes=[rT[0]])
                    kb.op("dve", lambda e: e.tensor_tensor(TTm[:, 0, :], NoT[:, 0, :], id2h, ALU.add), reads=[rNoT, rc], writes=[rTT[0]])
                    for l in range(1, 7):
                        a, b2 = (l - 1) % 2, l % 2
                        psA, rpsA = pd()
                        for hh in range(2):
                            kb.op("pe", lambda e: e.matmul(psA[:, hh * 128:(hh + 1) * 128], NoT[:, l, hh * 128:(hh + 1) * 128],
                                                           Tm[:, a, hh * 128:(hh + 1) * 128], start=True, stop=True),
                                  reads=[rNoT, rT[a]], writes=[rpsA])
                        kb.op("act", lambda e: e.copy(A1[:], psA[:]), reads=[rpsA], writes=[rA1])
                        psB, rpsB = pd()
                        for hh in range(2):
                            kb.op("pe", lambda e: e.matmul(psB[:, hh * 128:(hh + 1) * 128], No[:, l, hh * 128:(hh + 1) * 128],
                                                           TTm[:, a, hh * 128:(hh + 1) * 128], start=True, stop=True),
                                  reads=[rNo, rTT[a]], writes=[rpsB])
                        kb.op("dve", lambda e: e.tensor_copy(B1[:], psB[:]), reads=[rpsB], writes=[rB1])
                        yield
                        if l < 6:
                            psT, rpsT = pd()
                            for hh in range(2):
                                o_ = psT[:, hh * 128:(hh + 1) * 128]
                                kb.op("pe", lambda e: e.matmul(o_, self.ident_b, Tm[:, a, hh * 128:(hh + 1) * 128], start=True, stop=False),
                                      reads=[rT[a], self.r_ident], writes=[rpsT])
                                kb.op("pe", lambda e: e.matmul(o_, TTm[:, a, hh * 128:(hh + 1) * 128], A1[:, hh * 128:(hh + 1) * 128],
                                                               start=False, stop=True), reads=[rTT[a], rA1], writes=[rpsT])
                            kb.op("act", lambda e: e.copy(Tm[:, b2, :], psT[:]), reads=[rpsT], writes=[rT[b2]])
                        psU, rpsU = pd()
                        for hh in range(2):
                            o_ = psU[:, hh * 128:(hh + 1) * 128]
                            kb.op("pe", lambda e: e.matmul(o_, self.ident_b, TTm[:, a, hh * 128:(hh + 1) * 128], start=True, stop=False),
                                  reads=[rTT[a], self.r_ident], writes=[rpsU])
                            kb.op("pe", lambda e: e.matmul(o_, Tm[:, a, hh * 128:(hh + 1) * 128], B1[:, hh * 128:(hh + 1) * 128],
                                                           start=False, stop=True), reads=[rT[a], rB1], writes=[rpsU])
                        kb.op("dve", lambda e: e.tensor_copy(TTm[:, b2, :], psU[:]), reads=[rpsU], writes=[rTT[b2]])
                        yield
                    psX, rpsX = pd()
                    for hh in range(2):
                        kb.op("pe", lambda e: e.matmul(psX[:, hh * 128:(hh + 1) * 128], TTm[:, 0, hh * 128:(hh + 1) * 128], Xb[:, hh, :],
                                                       start=True, stop=True), reads=[rTT[0], rXb], writes=[rpsX])
                    kb.op("act", lambda e: e.copy(Xf[:].rearrange("p h t -> p (h t)"), psX[:]), reads=[rpsX], writes=[rXf])
                    yield
                    psr, rpsr = pg()
                    for hh in range(2):
                        kb.op("pe", lambda e: e.matmul(psr[0:64, hh * 128:(hh + 1) * 128], Xf[:, hh, 0:64], ArbT[:, hh, :],
                                                       start=True, stop=False),
                              reads=[rXf, rArb], writes=[rpsr])
                        kb.op("pe", lambda e: e.matmul(psr[0:64, hh * 128:(hh + 1) * 128], self.ident_b[:, HS[hh]], Rt[:, cs],
                                                       start=False, stop=True),
                              reads=[rrow["Rt"], self.r_ident], writes=[rpsr])
                    kb.op("act", lambda e: e.copy(Rh[:].rearrange("p h t -> p (h t)"), psr[0:64, :]), reads=[rpsr], writes=[rRh])
                    psm, rpsm = pg()
                    pcx = cj + 127 if slot == 0 else cj
                    for hh in range(2):
                        kb.op("pe", lambda e: e.matmul(psm[0:64, hh * 64:(hh + 1) * 64], Xf[:, hh, 0:64], TM[:, 1, HS[hh]],
                                                       start=True, stop=True),
                              reads=[rXf, rTM], writes=[rpsm])
                        kb.op("pe", lambda e: e.matmul(psm[0:64, 128 + hh:129 + hh], self.ident_f[:, HS[hh]], L[:, pcx:pcx + 1],
                                                       start=True, stop=True),
                              reads=[rL, self.r_ident], writes=[rpsm])
                    kb.op("dve", lambda e: e.tensor_copy(MTs[:].rearrange("p h t -> p (h t)"), psm[0:64, 0:128]),
                          reads=[rpsm], writes=[rMT])
                    kb.op("dve", lambda e: e.tensor_copy(pC2[:], psm[0:64, 128:130]), reads=[rpsm], writes=[rMT])
                    yield

                def chunk_tail(oi, j):
                    cj = CH_COLS[j]
                    cs = slice(cj, cj + 128)
                    T_ = csets[oi % 2]
                    AkT, ArbT, ArkT, TM, Xf, Xb, No, NoT, Tm, TTm, A1, B1, MTs, pC2, Rh = (T_[n] for n in (
                        "AkT", "ArbT", "ArkT", "TM", "Xf", "Xb", "No", "NoT", "Tm", "TTm", "A1", "B1", "MTs", "pC2", "Rh"))
                    rAk, rArb, rArk, rTM, rXf, rXb, rMT, rRh, rNo, rNoT, rA1, rB1, rT, rTT = (T_[n] for n in (
                        "rAk", "rArb", "rArk", "rTM", "rXf", "rXb", "rMT", "rRh", "rNo", "rNoT", "rA1", "rB1", "rT", "rTT"))
                    if slot == 1 and oi == 2:
                        kb.dma("sp", Hf[:], state_in[c * 128:(c + 1) * 128, :].rearrange("(hh k) v -> k hh v", hh=2), writes=[rHf])
                        kb.op("act", lambda e: e.copy(Hb[:], Hf[:]), reads=[rHf], writes=[rHb])
                    b = oi % 2
                    psy = PH[0:64, b, 0:256]
                    psh = PHh[0:64, 0, 0:128]
                    yh_mode = os.environ.get('SCAN_YH', 'yh')
                    for hh in range(2 if 'y' in yh_mode else 0):
                        yo = psy[:, hh * 128:(hh + 1) * 128]
                        kb.op("pe", lambda e: e.matmul(yo, Hb[:, hh, :], Rh[:, hh, :], start=True, stop=False),
                              reads=[rHb, rRh], writes=[rPH[b]])
                        kb.op("pe", lambda e: e.matmul(yo, Xf[:, hh, 64:128], ArbT[:, hh, :], start=False, stop=False),
                              reads=[rXf, rArb], writes=[rPH[b]])
                        kb.op("pe", lambda e: e.matmul(yo, TM[:, 3, HS[hh]], ArkT[:, hh, :], start=False, stop=True),
                              reads=[rTM, rArk], writes=[rPH[b]])
                    for hh in range(2 if 'h' in yh_mode else 0):
                        ho = psh[:, hh * 64:(hh + 1) * 64]
                        kb.op("pe", lambda e: e.matmul(ho, MTs[:, hh, :], Hb[:, hh, :], start=True, stop=False),
                              reads=[rMT, rHb], writes=[rPHh])
                        kb.op("pe", lambda e: e.matmul(ho, TM[:, 1, HS[hh]], Xf[:, hh, 64:128], start=False, stop=False),
                              reads=[rTM, rXf], writes=[rPHh])
                        kb.op("pe", lambda e: e.matmul(ho, TM[:, 2, HS[hh]], TM[:, 3, HS[hh]], start=False, stop=True),
                              reads=[rTM], writes=[rPHh])
                    if 'y' in yh_mode:
                        kb.op("act", lambda e: e.copy(ych[:, b].rearrange("p h t -> p (h t)"), psy), reads=[rPH[b]], writes=[rych[b]])
                    if 'd' in yh_mode or yh_mode == 'yh':
                      kb.dma("sp", yT[c * 128:(c + 1) * 128, cs].rearrange("(hh v) t -> v hh t", hh=2), ych[:, b], reads=[rych[b]])
                    for hh in range(2 if 'h' in yh_mode else 0):
                        kb.op("dve", lambda e: e.scalar_tensor_tensor(out=Hf[:, hh, :], in0=Hf[:, hh, :], scalar=pC2[:, hh:hh + 1],
                                                                      in1=psh[:, hh * 64:(hh + 1) * 64], op0=ALU.mult, op1=ALU.add),
                              reads=[rPHh, rMT], writes=[rHf])
                    kb.op("act", lambda e: e.copy(Hb[:], Hf[:]), reads=[rHf], writes=[rHb])

                if stop > 5:
                    for p0 in range(0, len(order), 2):
                        grp = [(oi, order[oi]) for oi in range(p0, min(p0 + 2, len(order)))]
                        gens = [chunk_pre(oi, j) for (oi, j) in grp]
                        live = list(gens)
                        while live:
                            for g_ in list(live):
                                try:
                                    next(g_)
                                except StopIteration:
                                    live.remove(g_)
                        for (oi, j) in grp:
                            chunk_tail(oi, j)
                if state_out is not None:
                    kb.dma("sp", state_out[c * 128:(c + 1) * 128, :].rearrange("(hh k) v -> k hh v", hh=2), Hf[:], reads=[rHf])
            kb.barrier()


GN_EPS = 64e-5
VEC_NAMES = (["c", "cctx"] + [f"modb{l}_{k}" for l in range(2) for k in range(6)] + ["n1g0", "n2g0", "n1g1", "n2g1"]
             + [f"mu{k}" for k in range(6)] + ["w0A", "w0B", "a0A", "a0B", "kk", "ka", "rk", "lnw", "lnb"])
VI = {n: i for i, n in enumerate(VEC_NAMES)}
NCST = 4 * 256 + 128 + 64 + 256
NLM = 2 * 7 * 256


class Builder5(Builder4):
    def stage(self, es, tag, dt, nb=3, w=512):
        t = es.enter_context(self.sbt(tag, [128, nb, w], dt))
        return t, [Res() for _ in range(nb)]

    def epi_resid(self, xT, l, kgate, st_out, st_in):
        nc, kb = _B(self)
        st, rs = st_out
        xin, rxin = st_in
        cnt = [0]

        def epi(oc0, m, pc0, n, ps, rps):
            b = cnt[0] % len(rs)
            cnt[0] += 1
            jj = 1 if pc0 < 258 else 0
            c = oc0 // 128
            kb.dma("sp", xin[0:m, b, 0:n], xT[oc0:oc0 + m, pc0:pc0 + n], writes=[rxin[b]])
            kb.op("dve", lambda e: e.scalar_tensor_tensor(out=st[0:m, b, 0:n], in0=ps, scalar=self.mcol(l, kgate, c, jj)[0:m, :],
                                                          in1=xin[0:m, b, 0:n], op0=ALU.mult, op1=ALU.add),
                  reads=[rps, rxin[b], self.r_mod], writes=[rs[b]])
            kb.dma("sp", xT[oc0:oc0 + m, pc0:pc0 + n], st[0:m, b, 0:n], reads=[rs[b]])
        return epi

    def epi_mul(self, dst, st_out, st_in):
        nc, kb = _B(self)
        st, rs = st_out
        xin, rxin = st_in
        cnt = [0]

        def epi(oc0, m, pc0, n, ps, rps):
            b = cnt[0] % len(rs)
            cnt[0] += 1
            kb.dma("sp", xin[0:m, b, 0:n], dst[oc0:oc0 + m, pc0:pc0 + n], writes=[rxin[b]])
            kb.op("dve", lambda e: e.tensor_tensor(st[0:m, b, 0:n], ps, xin[0:m, b, 0:n], ALU.mult),
                  reads=[rps, rxin[b]], writes=[rs[b]])
            kb.dma("sp", dst[oc0:oc0 + m, pc0:pc0 + n], st[0:m, b, 0:n], reads=[rs[b]])
        return epi

    def ffn(self, xT, l, wg, wu, wd, h2_d, hid_d, tiles):
        nc, kb = _B(self)
        from contextlib import ExitStack
        hv = dram_fm(h2_d)

        def consume(ti, pc0, n, h, rh):
            kb.dma("sp", hv[:, :, pc0:pc0 + n], h, reads=[rh])
        self.norm_tiles(xT, l, 4, 3, tiles, consume, out_dt=BF16)
        with ExitStack() as es:
            so = self.stage(es, "f_so", BF16)
            si = self.stage(es, "f_si", BF16)
            self.gemm(h2_d, D, [(wg, DFF, self.epi_store(hid_d, so, func=AF.Silu)),
                                (wu, DFF, self.epi_mul(hid_d, so, si))], tiles, tag="fg")
        groups = [tiles[i:i + 2] for i in range(0, len(tiles), 2)]
        for gi, grp in enumerate(groups):
            with ExitStack() as es:
                so = self.stage(es, "d_so", F32, nb=2)
                si = self.stage(es, "d_si", F32, nb=2)
                self.gemm(hid_d, DFF, [(wd, D, self.epi_resid(xT, l, 5, so, si))], grp, wcols=256, tag=f"fd{gi}")

    def rwkv_readout(self, yT0, yT1, bonusT, gT, zT, consts):
        nc, kb = _B(self)
        bones, rc = consts["bones"], consts["rc"]
        with self.sbt("ro_y", [128, NTP], F32) as y, self.sbt("ro_y1", [128, NTP], F32) as y1, \
                self.sbt("ro_b", [128, NTP], F32) as bo, self.sbt("ro_g", [128, NTP], F32) as g, \
                self.sbt("ro_d", [128, NTP], F32) as d, self.sbt("ro_s", [128, NTP], F32) as sq, \
                self.sbt("ro_z", [128, NTP], BF16) as z, self.pst("ro_ps", [128, 4, 512], F32) as PS:
            ry, ry1, rb, rg, rd, rs, rz = (Res() for _ in range(7))
            rp = [Res() for _ in range(4)]
            pi = [0]

            def pp():
                i = pi[0] % 4
                pi[0] += 1
                return PS[:, i, 0:256], rp[i]
            for c in range(16):
                rows = slice(c * 128, (c + 1) * 128)
                kb.dma("sp", y[:], yT0[rows, :], writes=[ry])
                kb.dma("sp", y1[:], yT1[rows, :], writes=[ry1])
                kb.dma("sp", bo[:], bonusT[rows, :], writes=[rb])
                kb.dma("sp", g[:], gT[rows, :], writes=[rg])
                kb.op("dve", lambda e: e.tensor_tensor(y[:], y[:], y1[:], ALU.add), reads=[ry1], writes=[ry])
                for p0 in range(0, NTP, 256):
                    n = min(256, NTP - p0)
                    ps, rps = pp()
                    kb.op("pe", lambda e: e.matmul(ps[:, 0:n], bones, y[:, p0:p0 + n], start=True, stop=True),
                          reads=[ry, rc], writes=[rps])
                    kb.op("dve", lambda e: e.scalar_tensor_tensor(out=d[:, p0:p0 + n], in0=ps[:, 0:n], scalar=-1.0 / 64,
                                                                  in1=y[:, p0:p0 + n], op0=ALU.mult, op1=ALU.add),
                          reads=[rps, ry], writes=[rd])
                kb.op("act", lambda e: e.activation(out=sq[:], in_=d[:], func=AF.Square), reads=[rd], writes=[rs])
                for p0 in range(0, NTP, 256):
                    n = min(256, NTP - p0)
                    ps, rps = pp()
                    kb.op("pe", lambda e: e.matmul(ps[:, 0:n], bones, sq[:, p0:p0 + n], start=True, stop=True),
                          reads=[rs, rc], writes=[rps])
                    kb.op("act", lambda e: e.activation(out=y1[:, p0:p0 + n], in_=ps[:, 0:n], func=AF.Sqrt, scale=1.0 / 64,
                                                        bias=GN_EPS), reads=[rps], writes=[ry1])
                kb.op("dve", lambda e: e.reciprocal(y1[:], y1[:]), writes=[ry1])
                kb.op("dve", lambda e: e.tensor_tensor(d[:], d[:], y1[:], ALU.mult), reads=[ry1], writes=[rd])
                kb.op("dve", lambda e: e.tensor_scalar(d[:], d[:], self.vcol(VI["lnw"], c), self.vcol(VI["lnb"], c),
                                                       ALU.mult, ALU.add), reads=[self.r_vfm], writes=[rd])
                kb.op("dve", lambda e: e.tensor_tensor(d[:], d[:], bo[:], ALU.add), reads=[rb], writes=[rd])
                kb.op("dve", lambda e: e.tensor_tensor(z[:], d[:], g[:], ALU.mult), reads=[rd, rg], writes=[rz])
                kb.dma("sp", zT[rows, :], z[:], reads=[rz])
            kb.barrier()

    def exchange_state(self, stateA, gath, state_in, sel_d):
        nc, kb = _B(self)
        kb.barrier()
        ccsem = kb._newsem("cc")
        with nc.Block() as blk:
            @blk.gpsimd
            def _(g):
                g.collective_compute("AllGather", ALU.bypass, replica_groups=[[0, 1], [2, 3], [4, 5], [6, 7]],
                                     ins=[stateA[:, :]], outs=[gath[:, :]]).then_inc(ccsem, 1)
                g.wait_ge(ccsem, 1)
        kb.waited["pool"][ccsem.name] = 1
        rg = Res()
        rg.w = {ccsem.name: (ccsem, 1, "cc")}
        with self.sbt("x_g", [128, 2, 16, 64], F32) as gt, self.sbt("x_s", [128, 2], F32) as sl:
            r1, r2 = Res(), Res()
            for i in range(2):
                kb.dma("sp", gt[:, i], gath[i * D:(i + 1) * D, :].rearrange("(c p) v -> p c v", p=128), reads=[rg], writes=[r1])
            kb.dma("sp", sl[:], sel_d.partition_broadcast(128), writes=[r2])
            kb.op("dve", lambda e: e.tensor_scalar(gt[:, 0], gt[:, 0], sl[:, 0:1], None, ALU.mult), reads=[r2], writes=[r1])
            kb.op("dve", lambda e: e.scalar_tensor_tensor(out=gt[:, 0], in0=gt[:, 1], scalar=sl[:, 1:2], in1=gt[:, 0],
                                                          op0=ALU.mult, op1=ALU.add), reads=[r2], writes=[r1])
            kb.dma("sp", state_in.rearrange("(c p) v -> p c v", p=128), gt[:, 0], reads=[r1])
            kb.barrier()


NEG_E05 = -0.6065306597126334
LAMBDA_INIT1 = 0.8 - 0.6 * 0.7408182206817179


def build_program(debug=(), upto="all"):
    from contextlib import ExitStack
    B = Builder6(debug=list(debug))
    self = B
    nc, kb = B.nc, B.kb
    I = {}
    for n, shp in (("ident", [128, 128]), ("vec_all", [len(VEC_NAMES) * 16, 128]), ("cst", [128, NCST]), ("lmask", [128, NLM]), ("cmask", [1, NTP]),
                   ("sel", [1, 2]), ("xo", [NOWN, D]), ("xc", [NCTX, D]), ("xh", [2, D]), ("mod_w", [2, D, 6 * D]),
                   ("rwkv_wr", [D, D]), ("rwkv_wk", [D, D]), ("rwkv_wv", [D, D]), ("rwkv_wo", [D, D]),
                   ("rwkv_w1", [2, D, LW]), ("rwkv_w2", [2, LW, D]), ("rwkv_a1", [2, D, LW]), ("rwkv_a2", [2, LW, D]),
                   ("rwkv_g1", [D, LG]), ("rwkv_g2", [LG, D]),
                   ("ffn_wg", [2, D, DFF]), ("ffn_wu", [2, D, DFF]), ("ffn_wd", [2, DFF, D])):
        I[n] = B.din(n, shp)
    out = nc.dram_tensor("out", [NOWN, D], F32, kind="ExternalOutput").ap()
    S = {}
    for n in ("xT", "rT", "kT", "vT", "gT", "ld0", "ld1", "al0", "al1", "yT0", "yT1", "bonusT"):
        S[n] = B.dscr(n, [D, NTP])
    for k in range(6):
        S[f"xmix{k}"] = B.dscr(f"xmix{k}", [D, NTP], BF16)
    for s_ in range(2):
        S[f"lw{s_}"] = B.dscr(f"lw{s_}", [LW, NTP], BF16)
        S[f"la{s_}"] = B.dscr(f"la{s_}", [LW, NTP], BF16)
    S["lg"] = B.dscr("lg", [LG, NTP], BF16)
    S["zT"] = B.dscr("zT", [D, NTP], BF16)
    S["h2"] = B.dscr("h2", [D, NTP], BF16)
    S["hid"] = B.dscr("hid", [DFF, NTP], BF16)
    S["stateA"] = B.dscr("stateA", [D, 64])
    S["gath"] = B.dscr("gath", [2 * D, 64])
    S["state_in"] = B.dscr("state_in", [D, 64])

    B.load_consts(I["ident"], None, I["vec_all"], len(VEC_NAMES))
    cs = nc.alloc_sbuf_tensor("cst_sb", [128, NCST], F32).ap()
    rc = Res()
    kb.dma("sp", cs, I["cst"], writes=[rc])
    lmt = nc.alloc_sbuf_tensor("lmask_sb", [128, NLM], BF16).ap()
    kb.dma("pool", lmt, I["lmask"], writes=[rc])
    consts = {"masks": cs[:, 0:1024].rearrange("p (m t) -> p m t", m=4), "rmask": None, "bones": cs[:, 1024:1152],
              "ident2": cs[:, 1152:1216], "id2h": cs[:, 1216:1472], "rc": rc,
              "lmask": lmt.rearrange("p (a l t) -> p a l t", a=2, l=7)}
    xT = S["xT"]
    B.rows_to_fm([(I["xc"], NCTX, 1), (I["xh"][0:1, :], 1, 258), (I["xo"], NOWN, 259), (I["xh"][1:2, :], 1, 2307)], xT)
    B.adaln(I["mod_w"], VI)
    with ExitStack() as es:
        cmt = es.enter_context(self.sbt("cmt", [128, NTP], F32))
        rcm = Res()
        kb.dma("sp", cmt[:], I["cmask"].partition_broadcast(128), writes=[rcm])
        tmp = es.enter_context(self.sbt("mx_tmp", [128, 16, 256], F32))
        xo_t = es.enter_context(self.sbt("mx_xo", [128, 2, 16, 256], BF16))
        mixer = B.make_mixer([S[f"xmix{k}"] for k in range(6)], VI, {"tmp": tmp, "xo": xo_t})
        B.norm_tiles(xT, 0, 1, 0, TT256, mixer, halo=1, cmask=(cmt[:], rcm), maxn=256)
    if upto == "mix":
        return B
    vb = lambda nm: (lambda c: B.vcol(VI[nm], c))

    def post_scale(v):
        def post(oc0, m, pc0, n, st_ap, rs):
            kb.op("dve", lambda e: e.tensor_scalar(st_ap, st_ap, v, None, ALU.mult), writes=[rs])
        return post
    with ExitStack() as es:
        sf = B.stage(es, "g_sf", F32)
        sb_ = B.stage(es, "g_sb", BF16)
        B.gemm(S["xmix0"], D, [(I["rwkv_wr"], D, B.epi_store(S["rT"], sf))], TT, tag="gr")
        B.gemm(S["xmix2"], D, [(I["rwkv_wk"], D, B.epi_store(S["kT"], sf))], TT, tag="gk")
        B.gemm(S["xmix3"], D, [(I["rwkv_wv"], D, B.epi_store(S["vT"], sf))], TT, tag="gv")
        B.gemm(S["xmix1"], D, [(I["rwkv_w1"][s_], LW, B.epi_store(S[f"lw{s_}"], sb_, func=AF.Tanh)) for s_ in range(2)], TT, tag="gw1")
        B.gemm(S["xmix4"], D, [(I["rwkv_a1"][s_], LW, B.epi_store(S[f"la{s_}"], sb_)) for s_ in range(2)], TT, tag="ga1")
        B.gemm(S["xmix5"], D, [(I["rwkv_g1"], LG, B.epi_store(S["lg"], sb_, func=AF.Sigmoid))], TT, tag="gg1")
        for s_, sl in enumerate("AB"):
            B.gemm(S[f"lw{s_}"], LW, [(I["rwkv_w2"][s_], D, B.epi_store(S[f"ld{s_}"], sf, func=AF.Sigmoid, bias=vb("w0" + sl),
                                                                        post=post_scale(NEG_E05)))], TT, tag=f"gw2{s_}")
            B.gemm(S[f"la{s_}"], LW, [(I["rwkv_a2"][s_], D, B.epi_store(S[f"al{s_}"], sf, func=AF.Sigmoid, bias=vb("a0" + sl)))],
                   TT, tag=f"ga2{s_}")
        B.gemm(S["lg"], LG, [(I["rwkv_g2"], D, B.epi_store(S["gT"], sf))], TT, tag="gg2")
    if upto == "proj":
        return B
    src0 = {"r": S["rT"], "k": S["kT"], "v": S["vT"], "ld": S["ld0"], "al": S["al0"], "alo": S["al1"]}
    src1 = {"r": S["rT"], "k": S["kT"], "v": S["vT"], "ld": S["ld1"], "al": S["al1"]}
    B.scan_pass(0, src0, VI, consts, S["yT0"], bonusT=S["bonusT"], state_out=S["stateA"])
    B.exchange_state(S["stateA"], S["gath"], S["state_in"], I["sel"])
    B.scan_pass(1, src1, VI, consts, S["yT1"], state_in=S["state_in"])
    B.rwkv_readout(S["yT0"], S["yT1"], S["bonusT"], S["gT"], S["zT"], consts)
    with ExitStack() as es:
        so = B.stage(es, "w_so", F32, nb=2)
        si = B.stage(es, "w_si", F32, nb=2)
        B.gemm(S["zT"], D, [(I["rwkv_wo"], D, B.epi_resid(xT, 0, 2, so, si))], TT, tag="gwo")
    if upto == "mixer0":
        return B
    B.ffn(xT, 0, I["ffn_wg"][0], I["ffn_wu"][0], I["ffn_wd"][0], S["h2"], S["hid"], TT)
    if "xT_l0" in B.debug:
        kb.dma("sp", B.dscr("xT_l0", [D, NTP]), xT)
        kb.barrier()
    if upto == "l0":
        return B
    D1 = {}
    for n, shp in (("gq8", [1, 512]), ("gk8", [1, 512]), ("lq1", [1, 64]), ("lk1", [1, 64]), ("lq2", [1, 64]), ("lk2", [1, 64]),
                   ("subg", [1, 128]), ("rope_cos", [NTP, 512]), ("rope_sin", [NTP, 512]),
                   ("diff_wqkv", [D, 3 * D]), ("diff_wo", [D, D])):
        D1[n] = B.din(n, shp)
    for n, shp in (("qT", [D, NOWN]), ("kT_ctx", [D, NCTX]), ("v_ctx", [NCTX, D])):
        D1[n] = B.dscr(n, shp, BF16)
    D1["kT_own"] = [B.dscr(f"kT_own{h}", [128, NOWN], BF16) for h in range(16)]
    D1["kT_all"] = [B.dscr(f"kT_all{h}", [256, NOWN], BF16) for h in range(16)]
    D1["v_own"] = [B.dscr(f"v_own{h}", [NOWN, 128], BF16) for h in range(16)]
    D1["v_all"] = [B.dscr(f"v_all{h}", [2 * NOWN, 128], BF16) for h in range(16)]
    hv1 = dram_fm(S["h2"])

    def consume1(ti, pc0, n, h, rh):
        kb.dma("sp", hv1[:, :, pc0:pc0 + n], h, reads=[rh])
    B.norm_tiles(xT, 1, 1, 0, TT, consume1, out_dt=BF16)
    B.qkv_phase(S["h2"], D1["diff_wqkv"], D1)
    rkv = B.exchange_kv(D1)
    B.attention(D1, rkv, S["zT"])
    with ExitStack() as es:
        so = B.stage(es, "w1_so", F32, nb=2)
        si = B.stage(es, "w1_si", F32, nb=2)
        B.gemm(S["zT"], D, [(D1["diff_wo"], D, B.epi_resid(xT, 1, 2, so, si))], TT[1:], tag="gwo1")
    B.ffn(xT, 1, I["ffn_wg"][1], I["ffn_wu"][1], I["ffn_wd"][1], S["h2"], S["hid"], TT[1:])
    B.fm_to_rows(xT, out)
    B.kb.barrier()
    return B


def _consts():
    idx = np.arange(128)
    mk = lambda f: np.tile(f(idx[None, :], idx[:, None]).astype(np.float32), (1, 2))
    masks = np.concatenate([mk(lambda f, p: f > p), mk(lambda f, p: f < p), mk(lambda f, p: f >= p), mk(lambda f, p: f <= p)], axis=1)
    bones = np.kron(np.eye(2), np.ones((64, 64))).astype(np.float32)
    ident2 = np.concatenate([np.eye(64), np.eye(64)], axis=0).astype(np.float32)
    id2h = np.concatenate([np.eye(128), np.eye(128)], axis=1).astype(np.float32)
    return np.ascontiguousarray(np.concatenate([masks, bones, ident2, id2h], axis=1))


def _lmask():
    idx = np.arange(128)
    out = np.zeros((128, 2, 7, 256), np.float32)
    for l in range(7):
        s = 2 ** l
        m = (((idx[:, None] // s) % 2 == 1) & ((idx[None, :] // s) == (idx[:, None] // s) - 1)).astype(np.float32)
        out[:, 0, l, :] = np.tile(m, (1, 2))
        out[:, 1, l, :] = np.tile(m.T, (1, 2))
    return np.ascontiguousarray(out.reshape(128, NLM))


def make_in_maps(inp):
    f = lambda a: np.ascontiguousarray(np.asarray(a, dtype=np.float32))
    x, c, ctx, c_ctx = f(inp["x"]), f(inp["c"]), f(inp["ctx"]), f(inp["c_ctx"])
    cst = _consts()
    ident = np.eye(128, dtype=np.float32)
    shared = {"ident": ident, "cst": cst, "lmask": _lmask(), "mod_w": f(inp["mod_w"]),
              "rwkv_wr": f(inp["rwkv_wr"][0]), "rwkv_wk": f(inp["rwkv_wk"][0]), "rwkv_wv": f(inp["rwkv_wv"][0]),
              "rwkv_wo": f(inp["rwkv_wo"][0]), "rwkv_g1": f(inp["rwkv_g1"][0]), "rwkv_g2": f(inp["rwkv_g2"][0]),
              "ffn_wg": f(inp["ffn_wg"]), "ffn_wu": f(inp["ffn_wu"]), "ffn_wd": f(inp["ffn_wd"])}
    dirw = {n: f(inp[n][0]) for n in ("rwkv_w1", "rwkv_w2", "rwkv_a1", "rwkv_a2", "rwkv_w0", "rwkv_a0")}
    dirw_sw = {n: np.ascontiguousarray(v[::-1]) for n, v in dirw.items()}
    maps = []
    for core in range(8):
        b, s = core // 2, core % 2
        own = x[b, s * NOWN:(s + 1) * NOWN]
        cx = ctx[b]
        if s == 1:
            own = own[::-1]
            cx = cx[::-1]
        xh = np.zeros((2, D), np.float32)
        xh[1] = x[b, NOWN] if s == 0 else x[b, NOWN - 1]
        cmask = np.ones((1, NTP), np.float32)
        cmask[0, [0, 257, 258]] = 0.0
        dw = dirw if s == 0 else dirw_sw
        vecs = {"c": c[b], "cctx": c_ctx}
        for l in range(2):
            for k in range(6):
                vecs[f"modb{l}_{k}"] = inp["mod_b"][l][k * D:(k + 1) * D]
            vecs[f"n1g{l}"] = inp["norm1_g"][l]
            vecs[f"n2g{l}"] = inp["norm2_g"][l]
        for k in range(6):
            vecs[f"mu{k}"] = inp["rwkv_mu"][0][k]
        vecs["w0A"], vecs["w0B"] = dw["rwkv_w0"][0], dw["rwkv_w0"][1]
        vecs["a0A"], vecs["a0B"] = dw["rwkv_a0"][0], dw["rwkv_a0"][1]
        vecs["kk"], vecs["ka"] = inp["rwkv_kk"][0], inp["rwkv_ka"][0]
        vecs["rk"] = np.asarray(inp["rwkv_rk"][0]).reshape(-1)
        vecs["lnw"], vecs["lnb"] = inp["rwkv_lnw"][0], inp["rwkv_lnb"][0]
        vec_all = np.ascontiguousarray(np.stack([f(vecs[n]) for n in VEC_NAMES]).reshape(-1, 128))
        m = dict(shared)
        m.update({"vec_all": vec_all, "cmask": cmask, "sel": np.array([[float(s == 1), float(s == 0)]], np.float32),
                  "xo": np.ascontiguousarray(own), "xc": np.ascontiguousarray(cx), "xh": xh,
                  "rwkv_w1": dw["rwkv_w1"], "rwkv_w2": dw["rwkv_w2"], "rwkv_a1": dw["rwkv_a1"], "rwkv_a2": dw["rwkv_a2"]})
        maps.append(m)
    return maps


TOK128 = [(1 + 128 * j, 128) for j in range(2)] + [(259 + 128 * j, 128) for j in range(16)]
NKT = 34


class Builder6(Builder5):
    def qkv_phase(self, h1_d, wqkv, D1):
        nc, kb = _B(self)
        from contextlib import ExitStack
        with ExitStack() as es:
            g8 = es.enter_context(self.sbt("q_g8", [128, 2, 512], F32))
            rg8 = Res()
            kb.dma("sp", g8[:, 0, :], D1["gq8"].partition_broadcast(128), writes=[rg8])
            kb.dma("sp", g8[:, 1, :], D1["gk8"].partition_broadcast(128), writes=[rg8])
            sq = es.enter_context(self.sbt("q_sq", [128, 512], F32))
            qn = es.enter_context(self.sbt("q_qn", [128, 2, 512], F32))
            rq = es.enter_context(self.sbt("q_rq", [128, 512], F32))
            ob = es.enter_context(self.sbt("q_ob", [128, 2, 512], BF16))
            cs_t = es.enter_context(self.sbt("q_cs", [128, 2, 2, 512], F32))
            ss = es.enter_context(self.sbt("q_ss", [128, 8], F32))
            tb = es.enter_context(self.sbt("q_tb", [64, 2, 8, 128], BF16))
            PT = es.enter_context(self.pst("q_pt", [128, 2, 512], F32))
            rsq, rss, rrq = Res(), Res(), Res()
            rqn, rob, rcs, rtb = [Res(), Res()], [Res(), Res()], [Res(), Res()], [Res(), Res()]
            rPT = [Res(), Res()]
            cnt = [0]

            def epi(col0, wc, pc0, n, ps, rps):
                isctx = pc0 < 258
                kind = col0 // 2048
                if kind == 0 and isctx:
                    return
                b = cnt[0] % 2
                cnt[0] += 1
                if kind == 2:
                    kb.op("act", lambda e: e.copy(ob[:, b, :], ps), reads=[rps], writes=[rob[b]])
                    cc = col0 - 4096
                    if isctx:
                        kb.dma("sp", D1["v_ctx"][pc0 - 1:pc0 - 1 + 128, cc:cc + 512], ob[:, b, :], reads=[rob[b]])
                    else:
                        for i4 in range(4):
                            kb.dma("sp", D1["v_own"][cc // 128 + i4][pc0 - 259:pc0 - 259 + 128, :], ob[:, b, i4 * 128:(i4 + 1) * 128],
                                   reads=[rob[b]])
                    return
                kb.op("act", lambda e: e.activation(out=sq[:], in_=ps, func=AF.Square), reads=[rps], writes=[rsq])
                kb.op("dve", lambda e: e.tensor_reduce(out=ss[:], in_=sq[:].rearrange("p (u d) -> p u d", d=64), axis=AX.X, op=ALU.add),
                      reads=[rsq], writes=[rss])
                kb.op("act", lambda e: e.activation(out=ss[:], in_=ss[:], func=AF.Sqrt, scale=1.0 / 64, bias=1e-6), writes=[rss])
                kb.op("dve", lambda e: e.reciprocal(ss[:], ss[:]), writes=[rss])
                if kind == 0:
                    kb.op("dve", lambda e: e.tensor_scalar(ss[:], ss[:], 0.125, None, ALU.mult), writes=[rss])
                kb.dma("sp", cs_t[:, b, 0, :], D1["rope_cos"][pc0:pc0 + 128, :], writes=[rcs[b]])
                kb.dma("sp", cs_t[:, b, 1, :], D1["rope_sin"][pc0:pc0 + 128, :], writes=[rcs[b]])
                for u in range(8):
                    us = slice(u * 64, (u + 1) * 64)
                    kb.op("dve", lambda e: e.scalar_tensor_tensor(out=qn[:, b, us], in0=ps[:, us], scalar=ss[:, u:u + 1],
                                                                  in1=g8[:, kind, us], op0=ALU.mult, op1=ALU.mult),
                          reads=[rps, rss, rg8], writes=[rqn[b]])
                v4 = lambda ap: ap.rearrange("p (a two f) -> p a two f", two=2, f=16)
                kb.op("dve", lambda e: e.tensor_scalar(v4(rq[:])[:, :, 0, :], v4(qn[:, b, :])[:, :, 1, :], -1.0, None, ALU.mult),
                      reads=[rqn[b]], writes=[rrq])
                kb.op("dve", lambda e: e.tensor_copy(v4(rq[:])[:, :, 1, :], v4(qn[:, b, :])[:, :, 0, :]),
                      reads=[rqn[b]], writes=[rrq])
                kb.op("dve", lambda e: e.tensor_tensor(qn[:, b, :], qn[:, b, :], cs_t[:, b, 0, :], ALU.mult),
                      reads=[rcs[b]], writes=[rqn[b]])
                kb.op("dve", lambda e: e.tensor_tensor(rq[:], rq[:], cs_t[:, b, 1, :], ALU.mult), reads=[rcs[b]], writes=[rrq])
                kb.op("dve", lambda e: e.tensor_tensor(ob[:, b, :], qn[:, b, :], rq[:], ALU.add), reads=[rrq, rqn[b]], writes=[rob[b]])
                for half in range(2):
                    for u4 in range(4):
                        u = half * 4 + u4
                        kb.op("pe", lambda e: e.matmul(PT[0:64, half, u4 * 128:(u4 + 1) * 128], ob[:, b, u * 64:(u + 1) * 64],
                                                       self.ident_b, start=True, stop=True),
                              reads=[rob[b], self.r_ident], writes=[rPT[half]])
                    if half == 0:
                        kb.op("act", lambda e: e.copy(tb[:, b, 0:4, :].rearrange("p u t -> p (u t)"), PT[0:64, 0, :]),
                              reads=[rPT[0]], writes=[rtb[b]])
                    else:
                        kb.op("dve", lambda e: e.tensor_copy(tb[:, b, 4:8, :].rearrange("p u t -> p (u t)"), PT[0:64, 1, :]),
                              reads=[rPT[1]], writes=[rtb[b]])
                u0 = (col0 % 2048) // 64
                if kind == 0:
                    dst = D1["qT"][u0 * 64:(u0 + 8) * 64, pc0 - 259:pc0 - 259 + 128]
                elif isctx:
                    dst = D1["kT_ctx"][u0 * 64:(u0 + 8) * 64, pc0 - 1:pc0 - 1 + 128]
                else:
                    for i4 in range(4):
                        dsth = D1["kT_own"][u0 // 2 + i4][:, pc0 - 259:pc0 - 259 + 128]
                        kb.dma("sp", dsth.rearrange("(u d) t -> d u t", d=64), tb[:, b, 2 * i4:2 * i4 + 2, :], reads=[rtb[b]])
                    return
                kb.dma("sp", dst.rearrange("(u d) t -> d u t", d=64), tb[:, b], reads=[rtb[b]])
            self.gemm(h1_d, D, [(wqkv, 3 * D, epi)], TT, x_stationary=True, tag="gqkv", npsum=4)

    def exchange_kv(self, D1):
        nc, kb = _B(self)
        kb.barrier()
        ccsem = kb._newsem("cckv")
        n = 0
        for h in range(16):
            for (a, b_) in ((D1["kT_own"][h], D1["kT_all"][h]), (D1["v_own"][h], D1["v_all"][h])):
                n += 1
                with nc.Block() as blk:
                    @blk.gpsimd
                    def _(g):
                        g.collective_compute("AllGather", ALU.bypass, replica_groups=[[0, 1], [2, 3], [4, 5], [6, 7]],
                                             ins=[a[:, :]], outs=[b_[:, :]]).then_inc(ccsem, 1)
                        g.wait_ge(ccsem, n)
        kb.waited["pool"][ccsem.name] = n
        r = Res()
        r.w = {ccsem.name: (ccsem, n, "cc")}
        return r

    def attention(self, D1, rkv, oT):
        nc, kb = _B(self)
        from contextlib import ExitStack
        with ExitStack() as es:
            sbt = lambda n, shp, dt: es.enter_context(self.sbt(n, shp, dt))
            KT = sbt("at_kt", [64, 2, NKT * 128], BF16)
            QT = sbt("at_qt", [64, 2, NOWN], BF16)
            Vs = sbt("at_v", [128, NKT, 129], BF16)
            PTs = sbt("at_p", [128, 2, 512], BF16)
            lv = sbt("at_lv", [128, 4, 64], F32)
            lam = sbt("at_lam", [128, 4], F32)
            sg = sbt("at_sg", [128, 128], F32)
            zz = sbt("at_z", [128, 4], F32)
            o0 = sbt("at_o0", [128, 128], F32)
            o1 = sbt("at_o1", [128, 128], F32)
            ob = sbt("at_ob", [128, 2, 128], BF16)
            ot = sbt("at_ot", [128, 2, 128], BF16)
            junk = sbt("at_junk", [128, 128], F32)
            PS = es.enter_context(self.pst("at_ps", [128, 2, 512], F32))
            PA = es.enter_context(self.pst("at_pa", [128, 4, 512], F32))
            PO = es.enter_context(self.pst("at_po", [128, 1, 512], F32))
            rKT, rQT, rV, rlam, rsg, rz, ro0, ro1, rPO, rj = (Res() for _ in range(10))
            rP, rPS, rob, rot = [Res(), Res()], [Res(), Res()], [Res(), Res()], [Res(), Res()]
            rPA = [Res() for _ in range(4)]
            for i, nmv in enumerate(("lq1", "lk1", "lq2", "lk2")):
                kb.dma("sp", lv[:, i, :], D1[nmv].partition_broadcast(128), writes=[rlam])
            kb.dma("sp", sg[:], D1["subg"].partition_broadcast(128), writes=[rsg])
            kb.op("dve", lambda e: e.tensor_scalar(sg[:], sg[:], 1.0 - LAMBDA_INIT1, None, ALU.mult), writes=[rsg])
            for i in range(2):
                kb.op("dve", lambda e: e.tensor_tensor(lv[:, 2 * i, :], lv[:, 2 * i, :], lv[:, 2 * i + 1, :], ALU.mult), writes=[rlam])
                kb.op("dve", lambda e: e.tensor_reduce(out=lam[:, i:i + 1], in_=lv[:, 2 * i, :], axis=AX.X, op=ALU.add), writes=[rlam])
            kb.op("act", lambda e: e.activation(out=lam[:, 0:2], in_=lam[:, 0:2], func=AF.Exp), writes=[rlam])
            kb.op("dve", lambda e: e.tensor_tensor(lam[:, 2:3], lam[:, 0:1], lam[:, 1:2], ALU.subtract), writes=[rlam])
            kb.op("dve", lambda e: e.tensor_scalar(lam[:, 3:4], lam[:, 2:3], LAMBDA_INIT1, -1.0, ALU.add, ALU.mult), writes=[rlam])
            kb.op("dve", lambda e: e.memset(Vs[:, :, 128:129], 1.0), writes=[rV])
            tcount = 0
            for h in range(16):
                for m in range(2):
                    rows = slice((2 * h + m) * 64, (2 * h + m + 1) * 64)
                    kb.dma("sp", KT[:, m, 0:256], D1["kT_ctx"][rows, :], writes=[rKT])
                    for r_ in range(2):
                        kb.dma("sp", KT[:, m, 256 + r_ * NOWN:256 + (r_ + 1) * NOWN], D1["kT_all"][h][r_ * 128 + m * 64:r_ * 128 + (m + 1) * 64, :],
                               reads=[rkv], writes=[rKT])
                    kb.dma("sp", QT[:, m, :], D1["qT"][rows, :], writes=[rQT])
                hc = slice(h * 128, (h + 1) * 128)
                kb.dma("sp", Vs[:, 0:2, 0:128], D1["v_ctx"][:, hc].rearrange("(kt p) e -> p kt e", p=128), writes=[rV])
                for r_ in range(2):
                    kb.dma("sp", Vs[:, 2 + 16 * r_:2 + 16 * (r_ + 1), 0:128],
                           D1["v_all"][h][r_ * NOWN:(r_ + 1) * NOWN, :].rearrange("(kt p) e -> p kt e", p=128), reads=[rkv], writes=[rV])
                for qb in range(4):
                    for m in range(2):
                        def smm(kt_, b_):
                            kb.op("pe", lambda e: e.matmul(PS[:, b_, :], KT[:, m, kt_ * 128:(kt_ + 1) * 128], QT[:, m, qb * 512:(qb + 1) * 512],
                                                           start=True, stop=True), reads=[rKT, rQT], writes=[rPS[b_]])
                        smm(0, tcount % 2)
                        for kt in range(NKT):
                            b = tcount % 2
                            tcount += 1
                            if kt + 1 < NKT:
                                smm(kt + 1, tcount % 2)
                            kb.op("act", lambda e: e.activation(out=PTs[:, b, :], in_=PS[:, b, :], func=AF.Exp),
                                  reads=[rPS[b]], writes=[rP[b]])
                            for qs in range(4):
                                bank = m * 2 + qs // 2
                                acc = PA[:, bank, (qs % 2) * 256:(qs % 2) * 256 + 129]
                                kb.op("pe", lambda e: e.matmul(acc, PTs[:, b, qs * 128:(qs + 1) * 128], Vs[:, kt, :],
                                                               start=(kt == 0), stop=(kt == NKT - 1)),
                                      reads=[rP[b], rV], writes=[rPA[bank]])
                    for qs in range(4):
                        a0 = PA[:, qs // 2, (qs % 2) * 256:(qs % 2) * 256 + 129]
                        a1 = PA[:, 2 + qs // 2, (qs % 2) * 256:(qs % 2) * 256 + 129]
                        ra0, ra1 = rPA[qs // 2], rPA[2 + qs // 2]
                        b = qs % 2
                        kb.op("dve", lambda e: e.tensor_copy(zz[:, 0:1], a0[:, 128:129]), reads=[ra0], writes=[rz])
                        kb.op("dve", lambda e: e.tensor_copy(zz[:, 1:2], a1[:, 128:129]), reads=[ra1], writes=[rz])
                        kb.op("dve", lambda e: e.reciprocal(zz[:, 0:2], zz[:, 0:2]), writes=[rz])
                        kb.op("dve", lambda e: e.tensor_tensor(zz[:, 1:2], zz[:, 1:2], lam[:, 3:4], ALU.mult), reads=[rlam], writes=[rz])
                        kb.op("dve", lambda e: e.tensor_scalar(o0[:], a0[:, 0:128], zz[:, 0:1], None, ALU.mult), reads=[ra0, rz], writes=[ro0])
                        kb.op("dve", lambda e: e.scalar_tensor_tensor(out=o1[:], in0=a1[:, 0:128], scalar=zz[:, 1:2], in1=o0[:],
                                                                      op0=ALU.mult, op1=ALU.add), reads=[ra1, rz, ro0], writes=[ro1])
                        kb.op("dve", lambda e: e.tensor_tensor(junk[:], o1[:], o1[:], ALU.mult), reads=[ro1], writes=[rj])
                        kb.op("dve", lambda e: e.tensor_reduce(out=zz[:, 2:3], in_=junk[:], axis=AX.X, op=ALU.add), reads=[rj], writes=[rz])
                        kb.op("act", lambda e: e.activation(out=zz[:, 2:3], in_=zz[:, 2:3], func=AF.Sqrt, scale=1.0 / 128, bias=1e-5), writes=[rz])
                        kb.op("dve", lambda e: e.reciprocal(zz[:, 2:3], zz[:, 2:3]), writes=[rz])
                        kb.op("dve", lambda e: e.scalar_tensor_tensor(out=ob[:, b, :], in0=o1[:], scalar=zz[:, 2:3], in1=sg[:],
                                                                      op0=ALU.mult, op1=ALU.mult), reads=[ro1, rz, rsg], writes=[rob[b]])
                        kb.op("pe", lambda e: e.matmul(PO[:, 0, 0:128], ob[:, b, :], self.ident_b, start=True, stop=True),
                              reads=[rob[b], self.r_ident], writes=[rPO])
                        kb.op("act", lambda e: e.copy(ot[:, b, :], PO[:, 0, 0:128]), reads=[rPO], writes=[rot[b]])
                        q0 = 259 + qb * 512 + qs * 128
                        kb.dma("sp", oT[hc, q0:q0 + 128], ot[:, b, :], reads=[rot[b]])
            kb.barrier()

    def fm_to_rows(self, xT, out):
        nc, kb = _B(self)
        xv = dram_fm(xT)
        with self.sbt("o_x", [128, 2, 16, 128], F32) as xt, self.sbt("o_r", [128, 2, 2048], F32) as rt, \
                self.pst("o_ps", [128, 4, 512], F32) as ps:
            rx, rr_ = [Res(), Res()], [Res(), Res()]
            rp = [Res() for _ in range(4)]
            for j in range(16):
                b = j % 2
                pc0 = 259 + j * 128
                kb.dma("sp", xt[:, b], xv[:, :, pc0:pc0 + 128], writes=[rx[b]])
                for q in range(4):
                    for i in range(4):
                        c = q * 4 + i
                        kb.op("pe", lambda e: e.matmul(ps[:, q, i * 128:(i + 1) * 128], xt[:, b, c, :], self.ident_f, start=True, stop=True),
                              reads=[rx[b], self.r_ident], writes=[rp[q]])
                    eng = "dve" if q % 2 == 0 else "act"
                    if eng == "dve":
                        kb.op("dve", lambda e: e.tensor_copy(rt[:, b, q * 512:(q + 1) * 512], ps[:, q, :]), reads=[rp[q]], writes=[rr_[b]])
                    else:
                        kb.op("act", lambda e: e.copy(rt[:, b, q * 512:(q + 1) * 512], ps[:, q, :]), reads=[rp[q]], writes=[rr_[b]])
                kb.dma("sp", out[j * 128:(j + 1) * 128, :], rt[:, b, :], reads=[rr_[b]])
            kb.barrier()


def _rope_tables(s):
    freqs = (10000.0 ** (-np.arange(16, dtype=np.float32) / 16)).astype(np.float32)
    cosE = np.ones((NTP, 64), np.float32)
    sinE = np.zeros((NTP, 64), np.float32)
    j = np.arange(NOWN)
    t = j if s == 0 else (2 * NOWN - 1 - j)
    row = (t // 64).astype(np.float32)[:, None] * freqs
    col = (t % 64).astype(np.float32)[:, None] * freqs
    cosE[259:259 + NOWN] = np.concatenate([np.cos(row), np.cos(row), np.cos(col), np.cos(col)], axis=1)
    sinE[259:259 + NOWN] = np.concatenate([np.sin(row), np.sin(row), np.sin(col), np.sin(col)], axis=1)
    return np.ascontiguousarray(np.tile(cosE, (1, 8))), np.ascontiguousarray(np.tile(sinE, (1, 8)))


def add_l1_maps(maps, inp):
    f = lambda a: np.ascontiguousarray(np.asarray(a, dtype=np.float32))
    ropes = [_rope_tables(0), _rope_tables(1)]
    extra = {"gq8": f(np.tile(np.asarray(inp["diff_qn"][0]), 8)[None, :]), "gk8": f(np.tile(np.asarray(inp["diff_kn"][0]), 8)[None, :]),
             "lq1": f(inp["diff_lq1"][0])[None, :], "lk1": f(inp["diff_lk1"][0])[None, :],
             "lq2": f(inp["diff_lq2"][0])[None, :], "lk2": f(inp["diff_lk2"][0])[None, :],
             "subg": f(inp["diff_subln"][0])[None, :], "diff_wqkv": f(inp["diff_wqkv"][0]), "diff_wo": f(inp["diff_wo"][0])}
    for core, m in enumerate(maps):
        m.update(extra)
        m["rope_cos"], m["rope_sin"] = ropes[core % 2]
    return maps


_PROG = None


def kernel(**inputs):
    global _PROG
    if _PROG is None:
        _PROG = build_program()
    B = _PROG
    maps = add_l1_maps(make_in_maps(inputs), inputs)
    used = set(B.inp.keys())
    maps = [{k: v for k, v in m.items() if k in used} for m in maps]
    res = run_bass_kernel_spmd(B.nc, maps, core_ids=list(range(8)))
    out = np.empty((4, 2 * NOWN, D), np.float32)
    for core in range(8):
        b, s = core // 2, core % 2
        o = np.asarray(res.results[core]["out"])
        out[b, s * NOWN:(s + 1) * NOWN] = o if s == 0 else o[::-1]
    return out
```
